# Optimizing a Trainium2 kernel written in Bass

```python
import math
import jax, jax.numpy as jnp
from jax import lax
import numpy as np

D_MODEL = 1024
BATCH = 8
SEQ = 4096
DEPTH = 1

MIX_WIDTH = D_MODEL
HEAD_DIM = 64
RWKV_WIDTH = MIX_WIDTH // 2
ATTN_WIDTH = MIX_WIDTH - RWKV_WIDTH
RWKV_HEADS = RWKV_WIDTH // HEAD_DIM
ATTN_HEADS = ATTN_WIDTH // HEAD_DIM
DECAY_RANK = 64
ICLR_RANK = 64
DILATION_PATTERNS = ((128, 1), (512, 4), (2048, 16))
ATTN_BLOCK = 128
ROPE_THETA = 500000.0
ROPE_DIM = HEAD_DIM // 4
NORM_EPS = 1e-6
GN_EPS = 64e-5
SHIFT_WIDTH = 3 * RWKV_WIDTH + DECAY_RANK + ICLR_RANK
IN_WIDTH = SHIFT_WIDTH + RWKV_WIDTH + 4 * ATTN_WIDTH

kernel_name = "hymba_rwkv7_dilated_swa_layer"


def _rmsnorm(x, g):
    xf = x.astype(jnp.float32)
    y = xf * lax.rsqrt(jnp.mean(xf * xf, axis=-1, keepdims=True) + NORM_EPS)
    return (y * g.astype(jnp.float32)).astype(x.dtype)


def _token_shift(p):
    return jnp.pad(p, ((0, 0), (1, 0), (0, 0)))[:, :-1]


def _rwkv7_scan(r, w, k, v, kk, b):
    bsz, _, nh, n = r.shape

    def step(S, inp):
        r_t, w_t, k_t, v_t, kk_t, b_t = inp
        sa = jnp.einsum('bhvk,bhk->bhv', S, -kk_t)
        S = (S * w_t[:, :, None, :] + sa[..., None] * b_t[:, :, None, :]
             + v_t[..., None] * k_t[:, :, None, :])
        y = jnp.einsum('bhvk,bhk->bhv', S, r_t)
        return S, y

    xs = (jnp.moveaxis(r, 1, 0), jnp.moveaxis(w, 1, 0), jnp.moveaxis(k, 1, 0),
          jnp.moveaxis(v, 1, 0), jnp.moveaxis(kk, 1, 0), jnp.moveaxis(b, 1, 0))
    S0 = jnp.zeros((bsz, nh, n, n), jnp.float32)
    _, ys = lax.scan(step, S0, xs)
    return jnp.moveaxis(ys, 0, 1)


def _rwkv7_mixer(p_r, p_k, p_v, p_w, p_a, decay_base, decay_up, iclr_base,
                 iclr_up, key_norm_scale, key_iclr_mix, bonus, gn_gain, gn_bias):
    bsz, t, _ = p_r.shape
    f32 = jnp.float32
    heads = lambda a: a.astype(f32).reshape(bsz, t, RWKV_HEADS, HEAD_DIM)
    decay_logit = -jax.nn.softplus(-(decay_base.astype(f32)
                                     + jnp.tanh(p_w.astype(f32)) @ decay_up.astype(f32))) - 0.5
    w = jnp.exp(-jnp.exp(decay_logit))
    a = jax.nn.sigmoid(iclr_base.astype(f32) + p_a.astype(f32) @ iclr_up.astype(f32))
    kk = heads(p_k * key_norm_scale)
    kk = kk / jnp.maximum(jnp.sqrt(jnp.sum(kk * kk, axis=-1, keepdims=True)), 1e-12)
    k = p_k.astype(f32) * (1.0 + (a - 1.0) * key_iclr_mix.astype(f32))
    r_h, k_h, v_h, w_h, a_h = heads(p_r), heads(k), heads(p_v), heads(w), heads(a)
    y = _rwkv7_scan(r_h, w_h, k_h, v_h, kk, kk * a_h)
    mu = jnp.mean(y, axis=-1, keepdims=True)
    var = jnp.mean(jnp.square(y - mu), axis=-1, keepdims=True)
    y = (y - mu) * lax.rsqrt(var + GN_EPS)
    y = y.reshape(bsz, t, RWKV_WIDTH) * gn_gain.astype(f32) + gn_bias.astype(f32)
    bonus_term = jnp.sum(r_h * k_h * bonus.astype(f32), axis=-1, keepdims=True) * v_h
    return y + bonus_term.reshape(bsz, t, RWKV_WIDTH)


def _rope_partial(t, pos):
    half = ROPE_DIM // 2
    inv = ROPE_THETA ** (-jnp.arange(half, dtype=jnp.float32) * 2.0 / ROPE_DIM)
    ang = pos.astype(jnp.float32)[:, None] * inv[None, :]
    cos = jnp.cos(ang)[None, :, None, :].astype(t.dtype)
    sin = jnp.sin(ang)[None, :, None, :].astype(t.dtype)
    x1, x2 = t[..., :half], t[..., half:ROPE_DIM]
    return jnp.concatenate([x1 * cos - x2 * sin, x1 * sin + x2 * cos, t[..., ROPE_DIM:]], axis=-1)


def _banded_attention(q, k, v, span):
    g, L, n = q.shape
    blk = ATTN_BLOCK
    nb = -(-L // blk)
    lp = nb * blk
    qb = jnp.pad(q, ((0, 0), (0, lp - L), (0, 0))).reshape(g, nb, blk, n)
    kf = jnp.pad(k, ((0, 0), (blk, lp - L), (0, 0)))
    vf = jnp.pad(v, ((0, 0), (blk, lp - L), (0, 0)))
    kw = jnp.concatenate([kf[:, :lp].reshape(g, nb, blk, n), kf[:, blk:].reshape(g, nb, blk, n)], axis=2)
    vw = jnp.concatenate([vf[:, :lp].reshape(g, nb, blk, n), vf[:, blk:].reshape(g, nb, blk, n)], axis=2)
    s = jnp.einsum('gnqd,gnkd->gnqk', qb, kw).astype(jnp.float32) * (n ** -0.5)
    i = jnp.arange(blk)[:, None]
    j = jnp.arange(2 * blk)[None, :]
    dist = i + blk - j
    bidx = jnp.arange(nb)[:, None, None]
    mask = (dist >= 0) & (dist <= span) & (bidx * blk + j - blk >= 0)
    s = jnp.where(mask, s, -jnp.inf)
    m = jnp.max(s, axis=-1, keepdims=True)
    p = jnp.exp(s - m)
    l = jnp.sum(p, axis=-1, keepdims=True)
    o = jnp.einsum('gnqk,gnkd->gnqd', (p / l).astype(v.dtype), vw)
    lse = (m + jnp.log(l))[..., 0]
    return o.reshape(g, lp, n)[:, :L], lse.reshape(g, lp)[:, :L]


def _dilated_attention(q, k, v):
    bsz, t, nh, n = q.shape
    outs, lses = [], []
    for window, dil in DILATION_PATTERNS:
        L = t // dil
        split = lambda a: a.reshape(bsz, L, dil, nh, n).transpose(0, 2, 3, 1, 4).reshape(bsz * dil * nh, L, n)
        o, lse = _banded_attention(split(q), split(k), split(v), window // dil)
        outs.append(o.reshape(bsz, dil, nh, L, n).transpose(0, 3, 1, 2, 4).reshape(bsz, t, nh, n))
        lses.append(lse.reshape(bsz, dil, nh, L).transpose(0, 3, 1, 2).reshape(bsz, t, nh))
    wts = jax.nn.softmax(jnp.stack(lses, axis=0), axis=0)
    out = jnp.sum(wts[..., None] * jnp.stack(outs, axis=0).astype(jnp.float32), axis=0)
    return out.astype(q.dtype)


def setup_inputs(seed: int = 0) -> dict:
    key = jax.random.key(seed)
    ks = jax.random.split(key, 16)
    f32 = jnp.float32
    nrm = lambda kk, shape: jax.random.normal(kk, shape, f32)
    x = nrm(ks[0], (BATCH, SEQ, D_MODEL))
    norm_gain = 1.0 + 0.02 * nrm(ks[1], (DEPTH, D_MODEL))
    w_in = nrm(ks[2], (DEPTH, D_MODEL, IN_WIDTH)) * D_MODEL ** -0.5
    shift_mix = jax.random.uniform(ks[3], (DEPTH, SHIFT_WIDTH), f32)
    decay_base = jax.random.uniform(ks[4], (DEPTH, RWKV_WIDTH), f32, minval=-4.0, maxval=1.0)
    decay_up = nrm(ks[5], (DEPTH, DECAY_RANK, RWKV_WIDTH)) * 0.5 * DECAY_RANK ** -0.5
    iclr_base = 0.1 * nrm(ks[6], (DEPTH, RWKV_WIDTH))
    iclr_up = nrm(ks[7], (DEPTH, ICLR_RANK, RWKV_WIDTH)) * 0.5 * ICLR_RANK ** -0.5
    key_norm_scale = 0.85 + 0.05 * nrm(ks[8], (DEPTH, RWKV_WIDTH))
    key_iclr_mix = 1.0 + 0.05 * nrm(ks[9], (DEPTH, RWKV_WIDTH))
    bonus = 0.1 * nrm(ks[10], (DEPTH, RWKV_HEADS, HEAD_DIM))
    gn_gain = 1.0 + 0.02 * nrm(ks[11], (DEPTH, RWKV_WIDTH))
    gn_bias = 0.02 * nrm(ks[12], (DEPTH, RWKV_WIDTH))
    w_out = nrm(ks[13], (DEPTH, MIX_WIDTH, D_MODEL)) * MIX_WIDTH ** -0.5
    final_gain = 1.0 + 0.02 * nrm(ks[14], (D_MODEL,))
    return {"x": x, "norm_gain": norm_gain, "w_in": w_in, "shift_mix": shift_mix,
            "decay_base": decay_base, "decay_up": decay_up, "iclr_base": iclr_base,
            "iclr_up": iclr_up, "key_norm_scale": key_norm_scale, "key_iclr_mix": key_iclr_mix,
            "bonus": bonus, "gn_gain": gn_gain, "gn_bias": gn_bias, "w_out": w_out,
            "final_gain": final_gain}


def reference(x, norm_gain, w_in, shift_mix, decay_base, decay_up, iclr_base, iclr_up,
              key_norm_scale, key_iclr_mix, bonus, gn_gain, gn_bias, w_out, final_gain):
    bsz, t, _ = x.shape
    pos = jnp.arange(t, dtype=jnp.int32)
    c_a, c_b = RWKV_WIDTH, ATTN_WIDTH
    for layer in range(DEPTH):
        h = _rmsnorm(x, norm_gain[layer])
        p = h @ w_in[layer]
        ps = p[..., :SHIFT_WIDTH]
        ps = ps + (_token_shift(ps) - ps) * shift_mix[layer]
        p_r = ps[..., :c_a]
        p_k = ps[..., c_a:2 * c_a]
        p_v = ps[..., 2 * c_a:3 * c_a]
        p_w = ps[..., 3 * c_a:3 * c_a + DECAY_RANK]
        p_a = ps[..., 3 * c_a + DECAY_RANK:SHIFT_WIDTH]
        o0 = SHIFT_WIDTH
        z_a = p[..., o0:o0 + c_a]
        o1 = o0 + c_a
        q = p[..., o1:o1 + c_b].reshape(bsz, t, ATTN_HEADS, HEAD_DIM)
        k = p[..., o1 + c_b:o1 + 2 * c_b].reshape(bsz, t, ATTN_HEADS, HEAD_DIM)
        v = p[..., o1 + 2 * c_b:o1 + 3 * c_b].reshape(bsz, t, ATTN_HEADS, HEAD_DIM)
        z_b = p[..., o1 + 3 * c_b:o1 + 4 * c_b]
        y_a = _rwkv7_mixer(p_r, p_k, p_v, p_w, p_a, decay_base[layer], decay_up[layer],
                           iclr_base[layer], iclr_up[layer], key_norm_scale[layer],
                           key_iclr_mix[layer], bonus[layer], gn_gain[layer], gn_bias[layer])
        y_a = y_a.astype(x.dtype) * jax.nn.silu(z_a)
        y_b = _dilated_attention(_rope_partial(q, pos), _rope_partial(k, pos), v)
        y_b = y_b.reshape(bsz, t, c_b) * jax.nn.silu(z_b)
        x = x + jnp.concatenate([y_a, y_b], axis=-1) @ w_out[layer]
    return _rmsnorm(x, final_gain)
```

```python
import math
from contextlib import ExitStack
import numpy as np
import ml_dtypes
import concourse.bass as bass
import concourse.mybir as mybir
from concourse.bass_utils import run_bass_kernel_spmd

F32 = mybir.dt.float32
BF16 = mybir.dt.bfloat16
ALU = mybir.AluOpType
AF = mybir.ActivationFunctionType
AX = mybir.AxisListType

T = 4096
D = 1024
INW = 4224
NT = 8
TT_ = 512
NEG_E = -math.exp(-0.5)
NORM_EPS = 1e-6
GN_EPS = 64e-5


class Buf:
    __slots__ = ('w', 'r', 'excl')

    def __init__(self):
        self.w = {}
        self.r = {}
        self.excl = False


class TL:
    __slots__ = ('t', 'b')

    def __init__(self, t):
        self.t = t
        self.b = Buf()

    def __getitem__(self, k):
        return self.t[k]


def _b(x):
    return x.b if isinstance(x, TL) else x


class Sched:
    ENG = ('tensor', 'vector', 'scalar', 'gpsimd', 'sync')

    def __init__(self, nc, sems, dma_sems):
        self.nc = nc
        self.sem = sems
        self.cnt = {e: 0 for e in self.ENG}
        self.ops = {e: [] for e in self.ENG}
        self.waited = {e: {} for e in self.ENG}
        self.dma_sems = dma_sems
        self.dma_cnt = [0] * len(dma_sems)
        self.dma_rr = 0
        self.dma_rr_q = {}

    def _wait(self, eng, key, val):
        if val <= 0 or self.waited[eng].get(key, 0) >= val:
            return
        self.waited[eng][key] = val
        sem = self.sem[key] if isinstance(key, str) else self.dma_sems[key]
        self.ops[eng].append(('wait', sem, val))

    def _deps(self, eng, reads, writes):
        for b in reads:
            for k, v in b.w.items():
                if k == eng and eng == 'tensor':
                    continue
                self._wait(eng, k, v)
            if b.excl:
                for k, v in b.r.items():
                    if k != eng:
                        self._wait(eng, k, v)
        for b in writes:
            for k, v in b.w.items():
                if k == eng and eng == 'tensor':
                    continue
                self._wait(eng, k, v)
            for k, v in b.r.items():
                if k == eng and eng == 'tensor':
                    continue
                self._wait(eng, k, v)

    def op(self, eng, fn, reads=(), writes=()):
        reads = [_b(x) for x in reads]
        writes = [_b(x) for x in writes]
        self._deps(eng, reads, writes)
        self.cnt[eng] += 1
        c = self.cnt[eng]
        self.ops[eng].append(('op', fn, self.sem[eng]))
        for b in reads:
            b.r[eng] = c
        for b in writes:
            b.w[eng] = c

    def dma(self, eng, fn, reads=(), writes=()):
        reads = [_b(x) for x in reads]
        writes = [_b(x) for x in writes]
        n = len(self.dma_sems) // 2
        base = 0 if eng == 'sync' else n
        rr = self.dma_rr_q.get(eng, 0)
        self.dma_rr_q[eng] = (rr + 1) % n
        i = base + rr
        self._wait(eng, i, self.dma_cnt[i])
        self._deps(eng, reads, writes)
        self.dma_cnt[i] += 16
        v = self.dma_cnt[i]
        self.ops[eng].append(('dma', fn, self.dma_sems[i]))
        for b in reads:
            b.r[i] = v
        for b in writes:
            b.w[i] = v

    def barrier(self):
        for e in self.ENG:
            for f in self.ENG:
                if f != e:
                    self._wait(e, f, self.cnt[f])
            for i in range(len(self.dma_sems)):
                self._wait(e, i, self.dma_cnt[i])

    def flush(self):
        nc = self.nc
        ops = self.ops

        def run(eng, lst):
            for o in lst:
                if o[0] == 'wait':
                    eng.wait_ge(o[1], o[2])
                elif o[0] == 'op':
                    o[1](eng).then_inc(o[2], 1)
                else:
                    o[1](eng).then_inc(o[2], 16)

        with nc.Block() as block:
            @block.tensor
            def _(e):
                run(e, ops['tensor'])

            @block.vector
            def _(e):
                run(e, ops['vector'])

            @block.scalar
            def _(e):
                run(e, ops['scalar'])

            @block.gpsimd
            def _(e):
                run(e, ops['gpsimd'])

            @block.sync
            def _(e):
                run(e, ops['sync'])
        self.ops = {e: [] for e in self.ENG}


class KB:
    def __init__(self, nc, S):
        self.nc = nc
        self.S = S
        self.rr = 0

    def mm(self, out, lhsT, rhs, start=True, stop=True, r=(), w=()):
        self.S.op('tensor', lambda e: e.matmul(out, lhsT=lhsT, rhs=rhs, start=start, stop=stop), r, w)

    def tr(self, out, in_, ident, r=(), w=()):
        self.S.op('tensor', lambda e: e.transpose(out=out, in_=in_, identity=ident), r, w)

    def act(self, out, in_, func, r=(), w=(), scale=1.0, bias=0.0, accum=None):
        if accum is None:
            self.S.op('scalar', lambda e: e.activation(out=out, in_=in_, func=func, bias=bias, scale=scale), r, w)
        else:
            self.S.op('scalar', lambda e: e.activation(out=out, in_=in_, func=func, bias=bias, scale=scale, accum_out=accum), r, w)

    def tt(self, out, in0, in1, op, r=(), w=(), eng='vector'):
        self.S.op(eng, lambda e: e.tensor_tensor(out=out, in0=in0, in1=in1, op=op), r, w)

    def ts(self, out, in0, s1, s2, op0, op1=None, r=(), w=(), eng='vector'):
        if op1 is None:
            self.S.op(eng, lambda e: e.tensor_scalar(out=out, in0=in0, scalar1=s1, scalar2=None, op0=op0), r, w)
        else:
            self.S.op(eng, lambda e: e.tensor_scalar(out=out, in0=in0, scalar1=s1, scalar2=s2, op0=op0, op1=op1), r, w)

    def stt(self, out, in0, scalar, in1, op0, op1, r=(), w=(), eng='vector'):
        self.S.op(eng, lambda e: e.scalar_tensor_tensor(out=out, in0=in0, scalar=scalar, in1=in1, op0=op0, op1=op1), r, w)

    def cp(self, out, in_, r=(), w=(), eng='vector'):
        if eng == 'scalar':
            self.act(out, in_, AF.Identity, r, w)
        else:
            self.S.op(eng, lambda e: e.tensor_copy(out=out, in_=in_), r, w)

    def cpa(self, out, in_, r=(), w=()):
        self.rr ^= 1
        self.cp(out, in_, r, w, eng='vector' if self.rr else 'scalar')

    def recip(self, out, in_, r=(), w=()):
        self.S.op('vector', lambda e: e.reciprocal(out=out, in_=in_), r, w)

    def red(self, out, in_, r=(), w=()):
        self.S.op('vector', lambda e: e.tensor_reduce(out=out, in_=in_, axis=AX.X, op=ALU.add), r, w)

    def memset(self, ap, val, w=(), eng='gpsimd'):
        self.S.op(eng, lambda e: e.memset(ap, val), (), w)

    def store(self, out, in_, r=(), w=()):
        self.S.dma('gpsimd', lambda e: e.dma_start(out=out, in_=in_), r, w)

    def dma(self, out, in_, r=(), w=(), eng='sync', slow=False):
        if slow:
            self.S.dma(eng, lambda e: e.dma_start(out=out, in_=in_, allow_slow_non_contiguous=True), r, w)
        else:
            self.S.dma(eng, lambda e: e.dma_start(out=out, in_=in_), r, w)


DBG = [99]
EXCL = [True]


def build_program(debug=False, phases=(0, 1, 2, 3), do_rwkv=True, n_tiles=NT):
    nc = bass.Bass("TRN2", target_bir_lowering=False)
    dk = "ExternalOutput" if debug else "Internal"

    def din(name, shape, dt=F32):
        return nc.dram_tensor(name, list(shape), dt, kind="ExternalInput").ap()

    x_d = din("x", [T, D])
    win_d = din("w_in", [D, INW])
    wout_d = din("w_out", [D, D])
    vecs_d = din("vecs", [128, 50])
    dupx_d = din("dupx", [65, 512])
    iup_d = din("iclr_up", [64, 512])
    gng_d = din("gn_g", [128, 512])
    gnb_d = din("gn_b", [128, 512])
    fg_d = din("fgain", [128, 1024])
    identb_d = din("ident_b", [128, 128], BF16)
    identf_d = din("ident_f", [128, 128])
    rswap_d = din("rswap", [128, 128], BF16)
    maskP_d = din("maskP", [128, 512], BF16)
    mask2_d = din("mask2", [128, 512])
    maskN_d = din("maskN", [128, 512])
    cm_d = din("cm", [128, 259])
    bones_d = din("bones", [128, 128])
    hsel_d = din("hsel", [128, 32])
    cos_d = din("cos_t", [16, T])
    sin_d = din("sin_t", [16, T])
    out_d = nc.dram_tensor("out", [T, D], F32, kind="ExternalOutput").ap()
    qT_d = nc.dram_tensor("qT_s", [512, T], BF16, kind=dk).ap()
    kT_d = nc.dram_tensor("kT_s", [512, T], BF16, kind=dk).ap()
    v_d = nc.dram_tensor("v_s", [T, 512], BF16, kind=dk).ap()
    gB_d = nc.dram_tensor("gB_s", [512, T], BF16, kind=dk).ap()
    yT_d = nc.dram_tensor("yT_s", [1024, T], BF16, kind=dk).ap()
    wsc_d = nc.dram_tensor("wsc_s", [33, 128, 1024], BF16, kind="Internal").ap()

    with ExitStack() as es:
        sems = {e: es.enter_context(nc.semaphore("s_" + e)) for e in Sched.ENG}
        dsems = [es.enter_context(nc.semaphore("d%d" % i)) for i in range(16)]
        S = Sched(nc, sems, dsems)
        K = KB(nc, S)

        def sb(ctx, name, shape, dt=F32):
            return TL(ctx.enter_context(nc.sbuf_tensor("sb_" + name, list(shape), dt)))

        def ps(ctx, name, shape, dt=F32):
            t = TL(ctx.enter_context(nc.psum_tensor("ps_" + name, list(shape), dt)))
            t.b.excl = EXCL[0]
            return t

        vecs = sb(es, "vecs", [128, 50])
        identb = sb(es, "identb", [128, 128], BF16)
        identf = sb(es, "identf", [128, 128])
        K.dma(vecs[:], vecs_d, w=[vecs])
        K.dma(identb[:], identb_d, w=[identb])
        K.dma(identf[:], identf_d, w=[identf])
        GAIN, MIX, OMM, IB, KNS, KIM, BON, NIB = 0, 8, 21, 34, 38, 42, 46, 34
        K.ts(vecs[:, OMM:OMM + 13], vecs[:, MIX:MIX + 13], -1.0, 1.0, ALU.mult, ALU.add, r=[vecs], w=[vecs])
        nib = sb(es, "nib", [128, 4])
        nkim = sb(es, "nkim", [128, 4])
        K.ts(nib[:], vecs[:, IB:IB + 4], -1.0, None, ALU.mult, r=[vecs], w=[nib])
        K.ts(nkim[:], vecs[:, KIM:KIM + 4], -1.0, None, ALU.mult, r=[vecs], w=[nkim])

        PB = [ps(es, "pb%d" % i, [128, 512]) for i in range(7)]
        PT = ps(es, "ptb", [128, 1024], BF16)

        if DBG[0] == 0:
            S.barrier()
            S.flush()
        if 1 in phases and DBG[0] >= 1:
            wscB = Buf()
            with ExitStack() as p0:
                stg = [sb(p0, "stg%d" % i, [128, 1408]) for i in range(2)]
                wo = [sb(p0, "wo%d" % i, [128, 1408], BF16) for i in range(2)]
                n = 0
                for kc in range(8):
                    for p3 in range(3):
                        st, wo_ = stg[n % 2], wo[n % 2]
                        K.dma(st[:], win_d[kc * 128:(kc + 1) * 128, p3 * 1408:(p3 + 1) * 1408], w=[st])
                        if n % 2 == 0:
                            K.ts(wo_[:], st[:], vecs[:, GAIN + kc:GAIN + kc + 1], None, ALU.mult, r=[st, vecs], w=[wo_])
                        else:
                            K.act(wo_[:], st[:], AF.Identity, r=[st, vecs], w=[wo_], scale=vecs[:, GAIN + kc:GAIN + kc + 1])
                        K.store(wsc_d[p3 * 11:(p3 + 1) * 11, :, kc * 128:(kc + 1) * 128].rearrange("j p c -> p j c"),
                                wo_[:].rearrange("p (j c) -> p j c", j=11), r=[wo_], w=[wscB])
                        n += 1
                S.barrier()
                S.flush()
            with ExitStack() as p1:
              if DBG[0] >= 2:
                phase1(nc, S, K, p1, sb, PB, PT, vecs, nib, nkim, identb, identf, wsc_d, wscB, x_d, dupx_d, iup_d, gng_d, gnb_d,
                       rswap_d, mask2_d, maskN_d, cm_d, bones_d, hsel_d, cos_d, sin_d, qT_d, kT_d, v_d, gB_d, yT_d,
                       do_rwkv, n_tiles)
                S.barrier()
                S.flush()
        if 2 in phases:
            with ExitStack() as p2:
                phase2(nc, S, K, p2, sb, PB, maskP_d, qT_d, kT_d, v_d, gB_d, yT_d)
                S.barrier()
                S.flush()
        if 3 in phases:
            with ExitStack() as p3:
                phase3(nc, S, K, p3, sb, PB, wout_d, fg_d, x_d, yT_d, out_d)
                S.barrier()
                S.flush()
    return nc


def phase1(nc, S, K, ctx, sb, PB, PT, vecs, nib, nkim, identb, identf, wsc_d, wscB, x_d, dupx_d, iup_d, gng_d, gnb_d,
           rswap_d, mask2_d, maskN_d, cm_d, bones_d, hsel_d, cos_d, sin_d, qT_d, kT_d, v_d, gB_d, yT_d,
           do_rwkv, n_tiles):
    GAIN, MIX, OMM, IB, KNS, KIM, BON = 0, 8, 21, 34, 38, 42, 46
    rswap = sb(ctx, "rswap", [128, 128], BF16)
    K.dma(rswap[:], rswap_d, w=[rswap])
    cosTs = [sb(ctx, "cosT%d" % i, [128, 512]) for i in range(2)]
    sinTs = [sb(ctx, "sinT%d" % i, [128, 512]) for i in range(2)]
    for i in range(2):
        K.memset(cosTs[i][:], 1.0, w=[cosTs[i]])
        K.memset(sinTs[i][:], 0.0, w=[sinTs[i]])
    if do_rwkv:
        dupx = sb(ctx, "dupx", [65, 512])
        iup = sb(ctx, "iup", [128, 512])
        gng = sb(ctx, "gng", [128, 512])
        gnb = sb(ctx, "gnb", [128, 512])
        mask2 = sb(ctx, "mask2", [128, 512])
        maskN = sb(ctx, "maskN", [128, 512])
        cm = sb(ctx, "cm", [128, 259])
        bones = sb(ctx, "bones", [128, 128])
        hsel = sb(ctx, "hsel", [128, 32])
        K.dma(dupx[:], dupx_d, w=[dupx])
        K.dma(iup[64:128, :], iup_d, w=[iup])
        K.dma(gng[:], gng_d, w=[gng])
        K.dma(gnb[:], gnb_d, w=[gnb])
        K.dma(mask2[:], mask2_d, w=[mask2])
        K.dma(maskN[:], maskN_d, w=[maskN])
        K.dma(cm[:], cm_d, w=[cm])
        K.dma(bones[:], bones_d, w=[bones])
        K.dma(hsel[:], hsel_d, w=[hsel])

    xt = [sb(ctx, "xt%d" % i, [128, 1024]) for i in range(2)]
    stat = [sb(ctx, "stat%d" % i, [128, 4]) for i in range(2)]
    hb = [sb(ctx, "hb%d" % i, [128, 1024], BF16) for i in range(2)]
    hTs = [sb(ctx, "hT%d" % i, [128, 8 * 512], BF16) for i in range(2)]
    tb = {}
    plast = sb(ctx, "plast", [128, 13])
    K.memset(plast[:], 0.0, w=[plast])
    qf = [sb(ctx, "qf0", [128, 512])] * 2
    q1 = [sb(ctx, "q1%d" % i, [128, 512], BF16) for i in range(2)]
    rtmp = sb(ctx, "rtmp", [128, 512])
    qb = [sb(ctx, "qb%d" % i, [128, 512], BF16) for i in range(2)]
    ge = [sb(ctx, "ge0", [128, 512])] * 2
    gb = [sb(ctx, "gb%d" % i, [128, 512], BF16) for i in range(2)]
    vtb = [sb(ctx, "vtb%d" % i, [128, 512], BF16) for i in range(2)]

    if do_rwkv:
        was = sb(ctx, "was", [128, 512])
        twx = sb(ctx, "twx", [65, 512])
        K.memset(twx[64:65, :], 1.0, w=[twx])
        sg = sb(ctx, "sg", [128, 4 * 512])
        rs = [sb(ctx, "rs%d" % i, [128, 512]) for i in range(2)]
        ks = [sb(ctx, "ks%d" % i, [128, 512]) for i in range(2)]
        vs = [sb(ctx, "vs%d" % i, [128, 512]) for i in range(2)]
        E1s = [sb(ctx, "E1_%d" % i, [128, 512]) for i in range(2)]
        E2s = [sb(ctx, "E2_%d" % i, [128, 512]) for i in range(2)]
        E3s = [sb(ctx, "E3_%d" % i, [128, 512]) for i in range(2)]
        decs = sb(ctx, "decs", [128, 4 * 4 * 3])
        avs = [sb(ctx, "av%d" % i, [128, 512]) for i in range(2)]
        svs = [sb(ctx, "sv%d" % i, [128, 512]) for i in range(2)]
        sqs = [sb(ctx, "sq%d" % i, [128, 512]) for i in range(2)]
        kp = sb(ctx, "kp", [128, 512])
        arT = [sb(ctx, "arT%d" % c, [128, 1024], BF16) for c in range(4)]
        kbT = [sb(ctx, "kbT%d" % c, [128, 1024], BF16) for c in range(4)]
        vbf = sb(ctx, "vbf", [128, 512], BF16)
        Ktok = sb(ctx, "Ktok", [128, 4 * 512], BF16)
        Btok = sb(ctx, "Btok", [128, 4 * 512], BF16)
        Vtok = sb(ctx, "Vtok", [128, 4 * 512], BF16)
        gA = sb(ctx, "gA", [128, 4 * 512], BF16)
        cbs = sb(ctx, "cbs", [128, 32])
        ABm = [sb(ctx, "ABm%d" % c, [128, 512], BF16) for c in range(4)]
        AKm = [sb(ctx, "AKm%d" % c, [128, 512], BF16) for c in range(4)]
        Nc = [[sb(ctx, "Nc%d_%d" % (g, i), [128, 512], BF16) for i in range(2)] for g in range(2)]
        Mc = [[sb(ctx, "Mc%d_%d" % (g, i), [128, 512], BF16) for i in range(2)] for g in range(2)]
        Pc = [[sb(ctx, "Pc%d_%d" % (g, i), [128, 512], BF16) for i in range(2)] for g in range(2)]
        St = sb(ctx, "St", [128, 256])
        K.memset(St[:], 0.0, w=[St])
        St0m = sb(ctx, "St0m", [128, 256], BF16)
        Xb = sb(ctx, "Xb", [128, 512], BF16)
        Ub = sb(ctx, "Ub", [128, 512], BF16)
        yv = sb(ctx, "yv", [128, 512])
        ysq = sb(ctx, "ysq", [128, 512])
        gst = sb(ctx, "gst", [128, 64])
        stmp = sb(ctx, "stmp", [128, 256])
        stmp2 = sb(ctx, "stmp2", [128, 256])
        yaT = [sb(ctx, "yaT%d" % i, [128, 512], BF16) for i in range(2)]

    nx = [0]

    def shift_evac(pb, j, dst):
        K.act(dst[:], pb[:], AF.Identity, r=[pb, vecs], w=[dst], scale=vecs[:, OMM + j:OMM + j + 1])
        K.stt(dst[:, 1:512], pb[:, 0:511], vecs[:, MIX + j:MIX + j + 1], dst[:, 1:512], ALU.mult, ALU.add, r=[pb, vecs, dst], w=[dst])
        K.stt(dst[:, 0:1], plast[:, j:j + 1], vecs[:, MIX + j:MIX + j + 1], dst[:, 0:1], ALU.mult, ALU.add, r=[plast, vecs, dst], w=[dst])
        K.cp(plast[:, j:j + 1], pb[:, 511:512], r=[pb], w=[plast])

    pj = [0]
    wch = [sb(ctx, "wch%d" % i, [128, 1024], BF16) for i in range(4)]
    wv = sb(ctx, "wv", [128, 4096], BF16)
    for cc in range(4):
        K.dma(wv[:, cc * 1024:(cc + 1) * 1024], wsc_d[25 + cc], r=[wscB], w=[wv])

    def proj(j):
        wb = wch[pj[0] % 4]
        pb = PB[pj[0] % 2]
        pj[0] += 1
        K.dma(wb[:], wsc_d[j], r=[wscB], w=[wb])
        if DBG[0] == 39:
            return pb
        for kc in range(8):
            K.mm(pb[:], wb[:, kc * 128:(kc + 1) * 128], tb['hT'][:, kc * 512:(kc + 1) * 512],
                 start=(kc == 0), stop=(kc == 7), r=[wb, tb['hT']], w=[pb])
        return pb

    def sigmoid_from_exp(dst, r=(), w=()):
        K.act(dst, dst, AF.Ln, r=r, w=w, bias=1.0)
        K.act(dst, dst, AF.Exp, r=r, w=w, scale=-1.0)

    PTf = PT[:].bitcast(F32)

    def q_gen(t0, hT, cosT, sinT):
        def projq(j):
            wb = wch[pj[0] % 4]
            pj[0] += 1
            pb = PB[6]
            K.dma(wb[:], wsc_d[j], r=[wscB], w=[wb])
            for kc in range(8):
                K.mm(pb[:], wb[:, kc * 128:(kc + 1) * 128], hT[:, kc * 512:(kc + 1) * 512],
                     start=(kc == 0), stop=(kc == 7), r=[wb, hT], w=[pb])
            return pb
        nq = 0
        for j, dst_d in [(17 + c, qT_d) for c in range(4)] + [(21 + c, kT_d) for c in range(4)]:
            c = (j - 17) % 4
            i2 = nq % 2
            nq += 1
            pb = projq(j)
            K.cp(qf[0][:], pb[:], r=[pb], w=[qf[0]], eng='scalar')
            K.cp(q1[i2][:], pb[:], r=[pb], w=[q1[i2]], eng='scalar')
            K.mm(PTf, rswap[:], q1[i2][:], r=[rswap, q1[i2]], w=[PT])
            K.tt(rtmp[:], PTf, sinT[:], ALU.mult, r=[PT, sinT], w=[rtmp])
            K.tt(qf[0][:], qf[0][:], cosT[:], ALU.mult, r=[qf[0], cosT], w=[qf[0]])
            K.tt(qb[i2][:], qf[0][:], rtmp[:], ALU.add, r=[qf[0], rtmp], w=[qb[i2]])
            K.store(dst_d[c * 128:(c + 1) * 128, t0:t0 + 512], qb[i2][:], r=[qb[i2]])
            yield
        for c in range(4):
            i2 = c % 2
            pb = projq(29 + c)
            K.act(ge[0][:], pb[:], AF.Exp, r=[pb], w=[ge[0]], scale=-1.0)
            sigmoid_from_exp(ge[0][:], r=[ge[0]], w=[ge[0]])
            K.tt(gb[i2][:], ge[0][:], pb[:], ALU.mult, r=[ge[0], pb], w=[gb[i2]])
            K.store(gB_d[c * 128:(c + 1) * 128, t0:t0 + 512], gb[i2][:], r=[gb[i2]])
            yield
        for tc in range(4):
            pb = PB[6]
            for cc in range(4):
                for kc in range(8):
                    K.mm(pb[:, cc * 128:(cc + 1) * 128], hT[:, kc * 512 + tc * 128: kc * 512 + (tc + 1) * 128],
                         wv[:, cc * 1024 + kc * 128: cc * 1024 + (kc + 1) * 128],
                         start=(kc == 0), stop=(kc == 7), r=[wv, hT], w=[pb])
            i2 = tc % 2
            K.cpa(vtb[i2][:], pb[:], r=[pb], w=[vtb[i2]])
            K.store(v_d[t0 + tc * 128: t0 + (tc + 1) * 128, :], vtb[i2][:], r=[vtb[i2]])
            yield

    qg = [None]

    def q_step(n=1):
        for _ in range(n):
            if qg[0] is None:
                return
            try:
                next(qg[0])
            except StopIteration:
                qg[0] = None
                return

    gn = [None]

    def gn_step(n=1):
        for _ in range(n):
            if gn[0] is None:
                return
            try:
                next(gn[0])
            except StopIteration:
                gn[0] = None
                return

    def gn_gen(tc, t0):
        y3 = yv[:].rearrange("p (h v) -> p h v", h=8)
        K.red(gst[:, 0:8], y3, r=[yv], w=[gst])
        K.act(ysq[:], yv[:], AF.Square, r=[yv], w=[ysq])
        K.red(gst[:, 8:16], ysq[:].rearrange("p (h v) -> p h v", h=8), r=[ysq], w=[gst])
        yield
        K.ts(gst[:, 16:24], gst[:, 0:8], 1.0 / 64, None, ALU.mult, r=[gst], w=[gst])
        K.tt(gst[:, 24:32], gst[:, 16:24], gst[:, 16:24], ALU.mult, r=[gst], w=[gst])
        K.stt(gst[:, 32:40], gst[:, 8:16], 1.0 / 64, gst[:, 24:32], ALU.mult, ALU.subtract, r=[gst], w=[gst])
        K.act(gst[:, 40:48], gst[:, 32:40], AF.Ln, r=[gst], w=[gst], bias=GN_EPS)
        K.act(gst[:, 48:56], gst[:, 40:48], AF.Exp, r=[gst], w=[gst], scale=-0.5)
        yield
        K.tt(y3, y3, gst[:, 16:24].unsqueeze(2).to_broadcast([128, 8, 64]), ALU.subtract, r=[yv, gst], w=[yv])
        K.tt(y3, y3, gst[:, 48:56].unsqueeze(2).to_broadcast([128, 8, 64]), ALU.mult, r=[yv, gst], w=[yv])
        yield
        K.tt(yv[:], yv[:], gng[:], ALU.mult, r=[yv, gng], w=[yv])
        K.tt(yv[:], yv[:], gnb[:], ALU.add, r=[yv, gnb], w=[yv])
        yield
        K.tt(ysq[:].rearrange("p (h v) -> p h v", h=8), Vtok[:, tc * 512:(tc + 1) * 512].rearrange("p (h v) -> p h v", h=8),
             cbs[:, tc * 8:(tc + 1) * 8].unsqueeze(2).to_broadcast([128, 8, 64]), ALU.mult, r=[Vtok, cbs], w=[ysq])
        K.tt(yv[:], yv[:], ysq[:], ALU.add, r=[yv, ysq], w=[yv])
        yield
        for c in range(4):
            K.tr(PTf[:, c * 128:(c + 1) * 128], yv[:, c * 128:(c + 1) * 128], identf[:], r=[yv, identf], w=[PT])
        yo_ = yaT[tc % 2]
        K.tt(yo_[:].rearrange("p (c t) -> p c t", c=4), PTf.rearrange("p (c t) -> p c t", c=4),
             gA[:].rearrange("p (c t) -> p c t", c=4)[:, :, tc * 128:(tc + 1) * 128], ALU.mult, r=[PT, gA], w=[yo_])
        K.store(yT_d[0:512, t0 + tc * 128: t0 + (tc + 1) * 128].rearrange("(c p) t -> p c t", p=128),
                yo_[:].rearrange("p (c t) -> p c t", c=4), r=[yo_])
        yield

    def head_gen(ti):
        t0 = ti * 512
        hT = hTs[ti % 2]
        cosT, sinT = cosTs[ti % 2], sinTs[ti % 2]
        for sub in range(4):
            i = nx[0] % 2
            nx[0] += 1
            xx = xt[i]
            stt_ = stat[i]
            K.dma(xx[:], x_d[t0 + sub * 128: t0 + (sub + 1) * 128, :], w=[xx])
            K.memset(stt_[:], 0.0, w=[stt_])
            K.act(hb[i][:], xx[:], AF.Square, r=[xx], w=[hb[i], stt_], accum=stt_[:, 0:1])
            K.act(stt_[:, 1:2], stt_[:, 0:1], AF.Ln, r=[stt_], w=[stt_], scale=1.0 / D, bias=NORM_EPS)
            K.act(stt_[:, 2:3], stt_[:, 1:2], AF.Exp, r=[stt_], w=[stt_], scale=-0.5)
            K.act(hb[i][:], xx[:], AF.Identity, r=[xx, stt_], w=[hb[i]], scale=stt_[:, 2:3])
            for kc in range(8):
                K.tr(PT[:, kc * 128:(kc + 1) * 128], hb[i][:, kc * 128:(kc + 1) * 128], identb[:], r=[hb[i], identb], w=[PT])
            K.cpa(hT[:].rearrange("p (k t) -> p k t", k=8)[:, :, sub * 128:(sub + 1) * 128],
                  PT[:].rearrange("p (k t) -> p k t", k=8), r=[PT], w=[hT])
            yield
        for base in (0, 64):
            K.dma(cosT[base:base + 16, :], cos_d[:, t0:t0 + 512], w=[cosT])
            K.dma(sinT[base:base + 16, :], sin_d[:, t0:t0 + 512], w=[sinT])
        if do_rwkv:
            wb = wch[pj[0] % 4]
            pj[0] += 1
            pb = PB[6]
            K.dma(wb[:], wsc_d[12], r=[wscB], w=[wb])
            for kc in range(8):
                K.mm(pb[:], wb[:, kc * 128:(kc + 1) * 128], hT[:, kc * 512:(kc + 1) * 512],
                     start=(kc == 0), stop=(kc == 7), r=[wb, hT], w=[pb])
            shift_evac(pb, 12, was)
            K.act(twx[0:64, :], was[0:64, :], AF.Exp, r=[was], w=[twx], scale=2.0)
            K.act(twx[0:64, :], twx[0:64, :], AF.Ln, r=[twx], w=[twx], bias=1.0)
            K.act(twx[0:64, :], twx[0:64, :], AF.Exp, r=[twx], w=[twx], scale=-1.0)
            K.ts(twx[0:64, :], twx[0:64, :], -2.0, 1.0, ALU.mult, ALU.add, r=[twx], w=[twx])
            yield
            for tc in range(4):
                pz = PB[6]
                K.mm(pz[:], twx[0:65, tc * 128:(tc + 1) * 128], dupx[0:65, :], r=[twx, dupx], w=[pz])
                dst = sg[:, tc * 512:(tc + 1) * 512]
                K.act(dst, pz[:], AF.Exp, r=[pz], w=[sg], scale=-1.0)
                sigmoid_from_exp(dst, r=[sg], w=[sg])
                yield

    hd = [head_gen(0)]

    def hd_step(n=1):
        for _ in range(n):
            if hd[0] is None:
                return
            try:
                next(hd[0])
            except StopIteration:
                hd[0] = None
                return

    for ti in range(n_tiles):
        t0 = ti * 512
        hd_step(1000)
        tb['hT'] = hTs[ti % 2]
        qg[0] = q_gen(t0, hTs[ti % 2], cosTs[ti % 2], sinTs[ti % 2])
        if ti + 1 < n_tiles:
            hd[0] = head_gen(ti + 1)
        if do_rwkv:
            if DBG[0] == 50:
                return
            def part_P(c):
                i2 = c % 2
                pb = proj(c)
                shift_evac(pb, c, rs[i2])
                pb = proj(4 + c)
                shift_evac(pb, 4 + c, ks[i2])
                pb = proj(8 + c)
                shift_evac(pb, 8 + c, vs[i2])

            def part_A(c):
                i2 = c % 2
                r_, k_, v_ = rs[i2], ks[i2], vs[i2]
                E1, E2, E3 = E1s[i2], E2s[i2], E3s[i2]
                av, sv, sq = avs[i2], svs[i2], sqs[i2]
                kkv, t1, rkb, bv = sv, sq, sq, av
                pza = PB[2]
                K.mm(pza[:], iup[64:128, c * 128:(c + 1) * 128], was[64:128, :], r=[iup, was], w=[pza])
                K.act(av[:], pza[:], AF.Exp, r=[pza, nib], w=[av], scale=-1.0, bias=nib[:, c:c + 1])
                sigmoid_from_exp(av[:], r=[av], w=[av])
                for tc in range(4):
                    pcs = PB[3 + tc % 2]
                    K.mm(pcs[:, 0:259], sg[:, tc * 512 + c * 128: tc * 512 + (c + 1) * 128], cm[:, 0:259], r=[sg, cm], w=[pcs])
                    K.act(E1[:, tc * 128:(tc + 1) * 128], pcs[:, 0:128], AF.Exp, r=[pcs], w=[E1])
                    K.act(E2[:, tc * 128:(tc + 1) * 128], pcs[:, 128:256], AF.Exp, r=[pcs], w=[E2])
                    K.act(E3[:, tc * 128:(tc + 1) * 128], pcs[:, 0:128], AF.Exp, r=[pcs], w=[E3], scale=-1.0)
                    o = (c * 4 + tc) * 3
                    K.act(decs[:, o:o + 3], pcs[:, 256:259], AF.Exp, r=[pcs], w=[decs])
                K.act(sv[:], k_[:], AF.Identity, r=[k_, vecs], w=[sv], scale=vecs[:, KNS + c:KNS + c + 1])
                K.act(sq[:], sv[:], AF.Square, r=[sv], w=[sq])
                pss = PB[2]
                K.mm(pss[:], bones[:], sq[:], r=[bones, sq], w=[pss])
                K.act(sq[:], pss[:], AF.Ln, r=[pss], w=[sq], bias=1e-30)
                K.act(sq[:], sq[:], AF.Exp, r=[sq], w=[sq], scale=-0.5)

            def part_D(c):
                i2 = c % 2
                r_, k_, v_ = rs[i2], ks[i2], vs[i2]
                E1, E2, E3 = E1s[i2], E2s[i2], E3s[i2]
                av, sv, sq = avs[i2], svs[i2], sqs[i2]
                kkv, t1, rkb, bv = sv, sq, sq, av
                K.tt(kkv[:], sv[:], sq[:], ALU.mult, r=[sv, sq], w=[kkv])
                K.act(t1[:], av[:], AF.Identity, r=[av, vecs, nkim], w=[t1], scale=vecs[:, KIM + c:KIM + c + 1], bias=nkim[:, c:c + 1])
                K.stt(kp[:], t1[:], 1.0, k_[:], ALU.add, ALU.mult, r=[t1, k_], w=[kp])
                K.tt(bv[:], kkv[:], av[:], ALU.mult, r=[kkv, av], w=[bv])
                K.stt(rkb[:], r_[:], vecs[:, BON + c:BON + c + 1], kp[:], ALU.mult, ALU.mult, r=[r_, vecs, kp], w=[rkb])
                pbon = PB[5]
                for tc in range(4):
                    K.mm(pbon[:, tc * 8 + 2 * c: tc * 8 + 2 * c + 2], rkb[:, tc * 128:(tc + 1) * 128], hsel[:, c * 8 + 2 * c: c * 8 + 2 * c + 2],
                         r=[rkb, hsel], w=[pbon])
                a4 = arT[c][:].rearrange("p (tc w t) -> p tc w t", tc=4, w=2)
                k4 = kbT[c][:].rearrange("p (tc w t) -> p tc w t", tc=4, w=2)
                v3 = lambda tl: tl[:].rearrange("p (tc t) -> p tc t", tc=4)
                K.stt(a4[:, :, 0, :], v3(kkv), -1.0, v3(E2), ALU.mult, ALU.mult, r=[kkv, E2], w=[arT[c]])
                K.tt(a4[:, :, 1, :], v3(r_), v3(E1), ALU.mult, r=[r_, E1], w=[arT[c]])
                K.tt(k4[:, :, 0, :], v3(kp), v3(E3), ALU.mult, r=[kp, E3], w=[kbT[c]])
                K.tt(k4[:, :, 1, :], v3(bv), v3(E3), ALU.mult, r=[bv, E3], w=[kbT[c]])
                K.cp(vbf[:], v_[:], r=[v_], w=[vbf], eng='scalar')
                for which, dstT in ((0, Ktok), (1, Btok), (2, Vtok)):
                    half = (c * 3 + which) % 2
                    pt = PT[:, half * 512:(half + 1) * 512]
                    for tc in range(4):
                        if which < 2:
                            src = kbT[c][:, tc * 256 + which * 128: tc * 256 + (which + 1) * 128]
                            rr = [kbT[c], identb]
                        else:
                            src = vbf[:, tc * 128:(tc + 1) * 128]
                            rr = [vbf, identb]
                        K.tr(pt[:, tc * 128:(tc + 1) * 128], src, identb[:], r=rr, w=[PT])
                    K.cpa(dstT[:].rearrange("p (tc ch) -> p tc ch", tc=4)[:, :, c * 128:(c + 1) * 128],
                          pt.rearrange("p (tc ch) -> p tc ch", tc=4), r=[PT], w=[dstT])
                pb = proj(13 + c)
                gdst = gA[:, c * 512:(c + 1) * 512]
                K.act(kp[:], pb[:], AF.Exp, r=[pb], w=[kp], scale=-1.0)
                sigmoid_from_exp(kp[:], r=[kp], w=[kp])
                K.tt(gdst, kp[:], pb[:], ALU.mult, r=[kp, pb], w=[gA])
                q_step(1)

            part_P(0)
            part_A(0)
            for c in range(4):
                if c + 1 < 4:
                    part_P(c + 1)
                    part_A(c + 1)
                part_D(c)
            if DBG[0] in (51, 52, 53):
                return
            K.cp(cbs[:], PB[5][:, 0:32], r=[PB[5]], w=[cbs])

            if DBG[0] == 54:
                return
            for tc in range(4):
                for c in range(4):
                    pH = [PB[(c % 2) * 2], PB[(c % 2) * 2 + 1]]
                    for half in range(2):
                        hs = slice(half * 64, half * 64 + 64)
                        ar = arT[c][hs, tc * 256:(tc + 1) * 256]
                        K.mm(pH[half][:, 0:256], kbT[c][hs, tc * 256 + 128: tc * 256 + 256], ar, r=[kbT[c], arT[c]], w=[pH[half]])
                        K.mm(pH[half][:, 256:512], kbT[c][hs, tc * 256: tc * 256 + 128], ar, r=[kbT[c], arT[c]], w=[pH[half]])
                    for half in range(2):
                        K.tt(ABm[c][:, half * 256:(half + 1) * 256], pH[half][:, 0:256], mask2[:, 0:256], ALU.mult, r=[pH[half], mask2], w=[ABm[c]])
                        K.tt(AKm[c][:, half * 256:(half + 1) * 256], pH[half][:, 256:512], mask2[:, 0:256], ALU.mult, r=[pH[half], mask2], w=[AKm[c]])
                if DBG[0] in (55, 551, 552):
                    continue
                q_step(1)
                gn_step(1)
                for g in range(2):
                    pNh = [PB[4], PB[5]]
                    for hh in range(4):
                        h = g * 4 + hh
                        c, half = h // 2, h % 2
                        hs = slice(half * 64, half * 64 + 64)
                        col = g * 256 + (hh // 2) * 128
                        K.mm(pNh[half][:, col:col + 128], arT[c][hs, tc * 256: tc * 256 + 128], kbT[c][hs, tc * 256 + 128: tc * 256 + 256],
                             r=[arT[c], kbT[c]], w=[pNh[half]])
                    for half in range(2):
                        K.tt(Nc[g][0][:].rearrange("p (pr h t) -> p pr h t", pr=2, h=2)[:, :, half, :],
                             pNh[half][:, g * 256:(g + 1) * 256].rearrange("p (pr t) -> p pr t", pr=2),
                             maskN[:, 0:256].rearrange("p (pr t) -> p pr t", pr=2), ALU.mult, r=[pNh[half], maskN], w=[Nc[g][0]])
                    for pr in range(2):
                        c = g * 2 + pr
                        src = ABm[c][:].rearrange("p (h w t) -> p h w t", h=2, w=2)[:, :, 0, :]
                        K.tt(Pc[g][0][:, pr * 256:(pr + 1) * 256].rearrange("p (h t) -> p h t", h=2), src,
                             identb[:].unsqueeze(1).to_broadcast([128, 2, 128]), ALU.add, r=[ABm[c], identb], w=[Pc[g][0]])
                if DBG[0] == 56:
                    continue
                def Mop(g, j, hh):
                    if j == 0:
                        h = g * 4 + hh
                        c, half = h // 2, h % 2
                        return ABm[c][:, half * 256: half * 256 + 128], ABm[c]
                    return Mc[g][j % 2][:, hh * 128:(hh + 1) * 128], Mc[g][j % 2]

                for j in range(6):
                    q_step(1)
                    gn_step(1)
                    if tc >= 1:
                        hd_step(1)
                    cur, nxt = j % 2, (j + 1) % 2
                    pNn = [PB[0], PB[3]]
                    pMn = [PB[1], PB[4]]
                    pP = [PB[2], PB[5]]
                    for g in range(2):
                        for hh in range(4):
                            sl = slice(hh * 128, (hh + 1) * 128)
                            m_ap, m_tl = Mop(g, j, hh)
                            K.mm(pNn[g][:, sl], m_ap, Nc[g][cur][:, sl], r=[m_tl, Nc[g][cur]], w=[pNn[g]])
                        if j < 5:
                            for hh in range(4):
                                sl = slice(hh * 128, (hh + 1) * 128)
                                m_ap, m_tl = Mop(g, j, hh)
                                K.mm(pMn[g][:, sl], Nc[g][cur][:, sl], m_ap, r=[m_tl, Nc[g][cur]], w=[pMn[g]])
                    for g in range(2):
                        K.cp(Nc[g][nxt][:], pNn[g][:], r=[pNn[g]], w=[Nc[g][nxt]], eng='scalar')
                        if j < 5:
                            K.cp(Mc[g][nxt][:], pMn[g][:], r=[pMn[g]], w=[Mc[g][nxt]], eng=('vector' if g == 0 else 'scalar'))
                    for g in range(2):
                        for hh in range(4):
                            sl = slice(hh * 128, (hh + 1) * 128)
                            K.mm(pP[g][:, sl], identb[:], Pc[g][cur][:, sl], start=True, stop=False, r=[identb, Pc[g][cur]], w=[pP[g]])
                            K.mm(pP[g][:, sl], Nc[g][nxt][:, sl], Pc[g][cur][:, sl], start=False, stop=True, r=[Nc[g][nxt], Pc[g][cur]], w=[pP[g]])
                    K.cp(Pc[0][nxt][:], pP[0][:], r=[pP[0]], w=[Pc[0][nxt]], eng='scalar')
                    K.cp(Pc[1][nxt][:], pP[1][:], r=[pP[1]], w=[Pc[1][nxt]], eng='vector')
                if DBG[0] == 57:
                    continue
                TTt = [Pc[0][0], Pc[1][0]]
                gn_step(1000)
                d4 = decs[:].rearrange("p (c tc k) -> p c tc k", c=4, tc=4)
                St3 = St[:].rearrange("p (c v) -> p c v", c=4)
                K.tt(St0m[:].rearrange("p (c v) -> p c v", c=4), St3, d4[:, :, tc, 0:1].to_broadcast([128, 4, 64]), ALU.mult,
                     r=[St, decs], w=[St0m])
                pX, pU, pY, pS = PB[0], PB[1], PB[2], PB[3]
                for h in range(8):
                    c, half = h // 2, h % 2
                    hs = slice(half * 64, half * 64 + 64)
                    vh = Vtok[:, tc * 512 + h * 64: tc * 512 + (h + 1) * 64]
                    K.mm(pX[:, h * 64:(h + 1) * 64], arT[c][hs, tc * 256: tc * 256 + 128], St0m[hs, c * 64:(c + 1) * 64],
                         start=True, stop=False, r=[arT[c], St0m], w=[pX])
                    K.mm(pX[:, h * 64:(h + 1) * 64], AKm[c][:, half * 256: half * 256 + 128], vh,
                         start=False, stop=True, r=[AKm[c], Vtok], w=[pX])
                K.cp(Xb[:], pX[:], r=[pX], w=[Xb], eng='scalar')
                q_step(1)
                for h in range(8):
                    g, hh = h // 4, h % 4
                    K.mm(pU[:, h * 64:(h + 1) * 64], TTt[g][:, hh * 128:(hh + 1) * 128], Xb[:, h * 64:(h + 1) * 64], r=[TTt[g], Xb], w=[pU])
                K.cp(Ub[:], pU[:], r=[pU], w=[Ub], eng='vector')
                q_step(1)
                for h in range(8):
                    c, half = h // 2, h % 2
                    hs = slice(half * 64, half * 64 + 64)
                    vh = Vtok[:, tc * 512 + h * 64: tc * 512 + (h + 1) * 64]
                    uh = Ub[:, h * 64:(h + 1) * 64]
                    yo = pY[:, h * 64:(h + 1) * 64]
                    K.mm(yo, arT[c][hs, tc * 256 + 128: tc * 256 + 256], St0m[hs, c * 64:(c + 1) * 64], start=True, stop=False, r=[arT[c], St0m], w=[pY])
                    K.mm(yo, ABm[c][:, half * 256 + 128: half * 256 + 256], uh, start=False, stop=False, r=[ABm[c], Ub], w=[pY])
                    K.mm(yo, AKm[c][:, half * 256 + 128: half * 256 + 256], vh, start=False, stop=True, r=[AKm[c], Vtok], w=[pY])
                    so = pS[hs, c * 64:(c + 1) * 64]
                    K.mm(so, Btok[:, tc * 512 + h * 64: tc * 512 + (h + 1) * 64], uh, start=True, stop=False, r=[Btok, Ub], w=[pS])
                    K.mm(so, Ktok[:, tc * 512 + h * 64: tc * 512 + (h + 1) * 64], vh, start=False, stop=True, r=[Ktok, Vtok], w=[pS])
                K.tt(stmp[:].rearrange("p (c v) -> p c v", c=4), pS[:, 0:256].rearrange("p (c v) -> p c v", c=4),
                     d4[:, :, tc, 2:3].to_broadcast([128, 4, 64]), ALU.mult, r=[pS, decs], w=[stmp])
                K.tt(stmp2[:].rearrange("p (c v) -> p c v", c=4), St3, d4[:, :, tc, 1:2].to_broadcast([128, 4, 64]), ALU.mult,
                     r=[St, decs], w=[stmp2])
                K.tt(St[:], stmp[:], stmp2[:], ALU.add, r=[stmp, stmp2], w=[St])
                if DBG[0] == 58:
                    continue
                q_step(1)
                K.cp(yv[:], pY[:], r=[pY], w=[yv], eng='scalar')
                gn[0] = gn_gen(tc, t0)
            gn_step(1000)

        if 50 <= DBG[0] <= 59 or DBG[0] in (551, 552):
            return
        q_step(1000)


def phase2(nc, S, K, ctx, sb, PB, maskP_d, qT_d, kT_d, v_d, gB_d, yT_d):
    maskP = sb(ctx, "maskP", [128, 512], BF16)
    K.dma(maskP[:], maskP_d, w=[maskP])
    qp = [sb(ctx, "qp%d" % i, [128, T], BF16) for i in range(2)]
    kp = [sb(ctx, "kpp%d" % i, [128, T], BF16) for i in range(2)]
    gp = [sb(ctx, "gp%d" % i, [128, T], BF16) for i in range(2)]
    Vd = [[sb(ctx, "Vd%d_%d" % (i, d), [128, 32 * 128], BF16) for d in range(3)] for i in range(2)]
    for i in range(2):
        for d in range(3):
            K.memset(Vd[i][d][:].rearrange("p (b c) -> p b c", c=128)[:, :, (1 - i) * 64:(1 - i) * 64 + 64], 1.0, w=[Vd[i][d]])
    acc = sb(ctx, "acc", [128, T])
    Pt = [sb(ctx, "Pt%d" % i, [128, 512], BF16) for i in range(4)]
    rden = sb(ctx, "rden", [128, T])
    yb = [sb(ctx, "ybo%d" % i, [128, T], BF16) for i in range(2)]
    DIL = (1, 4, 16)

    def loads(h):
        c, half = h // 2, h % 2
        pi = c % 2
        if half == 0:
            K.dma(qp[pi][:], qT_d[c * 128:(c + 1) * 128, :], w=[qp[pi]])
            K.dma(kp[pi][:], kT_d[c * 128:(c + 1) * 128, :], w=[kp[pi]])
            K.dma(gp[pi][:], gB_d[c * 128:(c + 1) * 128, :], w=[gp[pi]])
        vi = h % 2
        for di, d in enumerate(DIL):
            nb = 32 // d
            V3 = Vd[vi][di][:].rearrange("p (b c) -> p b c", c=128)
            for r in range(d):
                src = v_d[:, h * 64:(h + 1) * 64].rearrange("(n j r) c -> r j n c", r=d, j=128)[r]
                K.dma(V3[:, r * nb:(r + 1) * nb, half * 64:half * 64 + 64], src, w=[Vd[vi][di]])

    SB5 = [PB[0], PB[1], PB[2], PB[3], PB[6]]
    for pbk in SB5:
        K.memset(pbk[:], 0.0, w=[pbk], eng='vector')
    jobs = []
    for h in range(8):
        for di, d in enumerate(DIL):
            nb = 32 // d
            blocks = [(r, n) for n in range(nb) for r in range(d)] if d > 1 else [(0, n) for n in range(32)]
            for gi in range(8):
                grp = blocks[gi * 4:(gi + 1) * 4]
                for pair in range(2):
                    jobs.append(dict(h=h, di=di, d=d, nb=nb, gi=gi, pair=pair, grp=grp))
    for k, jb in enumerate(jobs):
        jb['k'] = k
        jb['first'] = (k == 0 or jobs[k - 1]['h'] != jb['h'])
        jb['last'] = (k == len(jobs) - 1 or jobs[k + 1]['h'] != jb['h'])

    def stage_S(jb):
        h, d, k = jb['h'], jb['d'], jb['k']
        c, half = h // 2, h % 2
        hs = slice(half * 64, half * 64 + 64)
        pi = c % 2
        q_, k_ = qp[pi], kp[pi]
        pS = SB5[k % 5]
        for bi in range(2):
            r, n = jb['grp'][jb['pair'] * 2 + bi]
            qs = q_[hs, r + d * 128 * n: r + d * 128 * n + d * 127 + 1: d]
            if n > 0:
                kprev = k_[hs, r + d * 128 * (n - 1): r + d * 128 * (n - 1) + d * 127 + 1: d]
                K.mm(pS[:, bi * 256: bi * 256 + 128], kprev, qs, r=[k_, q_], w=[pS])
            kcur = k_[hs, r + d * 128 * n: r + d * 128 * n + d * 127 + 1: d]
            K.mm(pS[:, bi * 256 + 128: bi * 256 + 256], kcur, qs, r=[k_, q_], w=[pS])

    def stage_E(jb):
        k = jb['k']
        pS = SB5[k % 5]
        P_ = Pt[k % 4]
        grp, pair = jb['grp'], jb['pair']
        K.act(P_[:], pS[:], AF.Exp, r=[pS], w=[P_], scale=0.125)
        K.tt(P_[:], P_[:], maskP[:], ALU.mult, r=[P_, maskP], w=[P_])

    def stage_V(jb):
        h, d, k, di, nb, gi = jb['h'], jb['d'], jb['k'], jb['di'], jb['nb'], jb['gi']
        c, half = h // 2, h % 2
        hs = slice(half * 64, half * 64 + 64)
        pi = c % 2
        vi = h % 2
        V3 = Vd[vi][di][:].rearrange("p (b c) -> p b c", c=128)
        P_ = Pt[k % 4]
        po = PB[4 + gi % 2]
        grp, pair = jb['grp'], jb['pair']
        for bi in range(2):
            r, n = grp[pair * 2 + bi]
            slot = pair * 2 + bi
            oo = po[:, slot * 128:(slot + 1) * 128]
            blk = r * nb + n
            if n > 0:
                K.mm(oo, V3[:, blk - 1, :], P_[:, bi * 256: bi * 256 + 128], start=True, stop=False, r=[Vd[vi][di], P_], w=[po])
            K.mm(oo, V3[:, blk, :], P_[:, bi * 256 + 128: bi * 256 + 256], start=(n == 0), stop=True, r=[Vd[vi][di], P_], w=[po])

    def stage_A(jb):
        h, d, k, di, nb, gi = jb['h'], jb['d'], jb['k'], jb['di'], jb['nb'], jb['gi']
        c, half = h // 2, h % 2
        hs = slice(half * 64, half * 64 + 64)
        pi = c % 2
        po = PB[4 + gi % 2]
        grp, pair = jb['grp'], jb['pair']
        if pair == 1:
            if d == 1:
                K.cp(acc[:, gi * 512:(gi + 1) * 512], po[:], r=[po], w=[acc], eng='scalar')
            else:
                r0, n = grp[0]
                span = d * 128
                av = acc[:, n * span:(n + 1) * span].rearrange("p (i r) -> p r i", r=d)[:, r0:r0 + 4, :]
                K.tt(av, av, po[:].rearrange("p (r i) -> p r i", r=4), ALU.add, r=[acc, po], w=[acc])
        if jb['last']:
            os_ = slice((1 - half) * 64, (1 - half) * 64 + 64)
            K.recip(rden[hs, :], acc[os_, :], r=[acc], w=[rden])
            K.tt(acc[hs, :], acc[hs, :], rden[hs, :], ALU.mult, r=[acc, rden], w=[acc])
            K.tt(yb[pi][hs, :], acc[hs, :], gp[pi][hs, :], ALU.mult, r=[acc, gp[pi]], w=[yb[pi]])
            if half == 1:
                K.store(yT_d[512 + c * 128: 512 + (c + 1) * 128, :], yb[pi][:], r=[yb[pi]])
            if h + 2 < 8:
                loads(h + 2)

    DEPTH = 3
    LAG = 2
    nj = len(jobs)
    loads(0)
    loads(1)
    for k in range(min(DEPTH, nj)):
        stage_S(jobs[k])
    for k in range(nj):
        if k + DEPTH < nj:
            stage_S(jobs[k + DEPTH])
        stage_E(jobs[k])
        stage_V(jobs[k])
        if k - LAG >= 0:
            stage_A(jobs[k - LAG])
    for k in range(max(0, nj - LAG), nj):
        stage_A(jobs[k])


def phase3(nc, S, K, ctx, sb, PB, wout_d, fg_d, x_d, yT_d, out_d):
    woutb = sb(ctx, "woutb", [128, 8 * 1024], BF16)
    stg = [sb(ctx, "wstg%d" % i, [128, 1024]) for i in range(2)]
    fg = sb(ctx, "fg", [128, 1024])
    K.dma(fg[:], fg_d, w=[fg])
    for kc in range(8):
        st = stg[kc % 2]
        K.dma(st[:], wout_d[kc * 128:(kc + 1) * 128, :], w=[st])
        K.cpa(woutb[:, kc * 1024:(kc + 1) * 1024], st[:], r=[st], w=[woutb])
    yt = [sb(ctx, "yt%d" % i, [128, 8 * 128], BF16) for i in range(3)]
    xr = [sb(ctx, "xr%d" % i, [128, 1024]) for i in range(3)]
    xo = [sb(ctx, "xo%d" % i, [128, 1024]) for i in range(2)]
    oo = [sb(ctx, "oo%d" % i, [128, 1024]) for i in range(2)]
    stat = [sb(ctx, "stat3_%d" % i, [128, 4]) for i in range(2)]
    for t in range(32):
        i3, i2 = t % 3, t % 2
        K.dma(yt[i3][:].rearrange("p (c t) -> p c t", c=8), yT_d[:, t * 128:(t + 1) * 128].rearrange("(c p) t -> p c t", p=128), w=[yt[i3]])
        K.dma(xr[i3][:], x_d[t * 128:(t + 1) * 128, :], w=[xr[i3]])
        for hf in range(2):
            pb = PB[(t * 2 + hf) % 4]
            for kc in range(8):
                K.mm(pb[:], yt[i3][:, kc * 128:(kc + 1) * 128], woutb[:, kc * 1024 + hf * 512: kc * 1024 + (hf + 1) * 512],
                     start=(kc == 0), stop=(kc == 7), r=[yt[i3], woutb], w=[pb])
            K.tt(xo[i2][:, hf * 512:(hf + 1) * 512], pb[:], xr[i3][:, hf * 512:(hf + 1) * 512], ALU.add, r=[pb, xr[i3]], w=[xo[i2]])
        st = stat[i2]
        K.memset(st[:], 0.0, w=[st])
        K.act(oo[i2][:], xo[i2][:], AF.Square, r=[xo[i2]], w=[oo[i2], st], accum=st[:, 0:1])
        K.act(st[:, 1:2], st[:, 0:1], AF.Ln, r=[st], w=[st], scale=1.0 / D, bias=NORM_EPS)
        K.act(st[:, 2:3], st[:, 1:2], AF.Exp, r=[st], w=[st], scale=-0.5)
        K.stt(oo[i2][:], xo[i2][:], st[:, 2:3], fg[:], ALU.mult, ALU.mult, r=[xo[i2], st, fg], w=[oo[i2]])
        K.store(out_d[t * 128:(t + 1) * 128, :], oo[i2][:], r=[oo[i2]])


def host_consts():
    bf = ml_dtypes.bfloat16
    c = {}
    c["ident_b"] = np.eye(128, dtype=np.float32).astype(bf)
    c["ident_f"] = np.eye(128, dtype=np.float32)
    rs = np.zeros((128, 128), np.float32)
    for p in range(128):
        m = p % 64
        if m < 8:
            rs[p + 8, p] = -1.0
        elif m < 16:
            rs[p - 8, p] = 1.0
    c["rswap"] = rs.astype(bf)
    j = np.arange(128)[:, None]
    i = np.arange(128)[None, :]
    U = (j >= i).astype(np.float32)
    L = (j <= i).astype(np.float32)
    c["maskP"] = np.concatenate([U, L, U, L], 1).astype(bf)
    SU = (j < i).astype(np.float32)
    UI = (j <= i).astype(np.float32)
    c["mask2"] = np.concatenate([SU, UI, SU, UI], 1)
    SL = (j > i).astype(np.float32)
    c["maskN"] = np.concatenate([SL, SL, SL, SL], 1)
    s = np.arange(128)[:, None]
    t = np.arange(128)[None, :]
    mid = (s <= 63).astype(np.float64)
    cm = np.zeros((128, 259), np.float64)
    cm[:, 0:128] = (s <= t) - mid
    cm[:, 128:256] = (s < t) - mid
    cm[:, 256:257] = mid
    cm[:, 257] = 1.0
    cm[:, 258:259] = 1.0 - mid
    c["cm"] = (cm * NEG_E).astype(np.float32)
    p = np.arange(128)
    c["bones"] = (p[:, None] // 64 == p[None, :] // 64).astype(np.float32)
    hsel = np.zeros((128, 4, 8), np.float32)
    for cc in range(4):
        for pp in range(128):
            hsel[pp, cc, 2 * cc + pp // 64] = 1.0
    c["hsel"] = hsel.reshape(128, 32)
    half = 8
    inv = (np.float32(500000.0) ** (-(np.arange(half, dtype=np.float32) * np.float32(2.0) / np.float32(16)))).astype(np.float32)
    ang = (np.arange(T, dtype=np.float32)[None, :] * inv[:, None]).astype(np.float32)
    cs = np.cos(ang).astype(np.float32)
    sn = np.sin(ang).astype(np.float32)
    c["cos_t"] = np.concatenate([cs, cs], 0)
    c["sin_t"] = np.concatenate([sn, sn], 0)
    return c


def pc(v, nchunk):
    return np.ascontiguousarray(np.asarray(v, np.float32).reshape(nchunk, 128).T)


def make_in_maps(inputs):
    c = host_consts()
    vecs = np.zeros((128, 50), np.float32)
    vecs[:, 0:8] = pc(inputs["norm_gain"][0], 8)
    vecs[:, 8:21] = pc(inputs["shift_mix"][0], 13)
    vecs[:, 34:38] = pc(inputs["iclr_base"][0], 4)
    vecs[:, 38:42] = pc(inputs["key_norm_scale"][0], 4)
    vecs[:, 42:46] = pc(inputs["key_iclr_mix"][0], 4)
    vecs[:, 46:50] = pc(inputs["bonus"][0].reshape(-1), 4)
    shared = dict(c)
    shared["w_in"] = np.ascontiguousarray(inputs["w_in"][0], np.float32)
    shared["w_out"] = np.ascontiguousarray(inputs["w_out"][0], np.float32)
    shared["vecs"] = vecs
    shared["dupx"] = np.ascontiguousarray(np.concatenate([inputs["decay_up"][0], inputs["decay_base"][0][None, :]], 0), np.float32)
    shared["iclr_up"] = np.ascontiguousarray(inputs["iclr_up"][0], np.float32)
    shared["gn_g"] = np.ascontiguousarray(np.broadcast_to(inputs["gn_gain"][0][None, :], (128, 512)), np.float32)
    shared["gn_b"] = np.ascontiguousarray(np.broadcast_to(inputs["gn_bias"][0][None, :], (128, 512)), np.float32)
    shared["fgain"] = np.ascontiguousarray(np.broadcast_to(np.asarray(inputs["final_gain"])[None, :], (128, 1024)), np.float32)
    maps = []
    for b in range(8):
        m = dict(shared)
        m["x"] = np.ascontiguousarray(inputs["x"][b], np.float32)
        maps.append(m)
    return maps


def kernel(**inputs):
    inputs = {k: np.asarray(v) for k, v in inputs.items()}
    nc = build_program()
    maps = make_in_maps(inputs)
    res = run_bass_kernel_spmd(nc, maps, core_ids=list(range(8)))
    out = np.stack([np.asarray(res.results[b]["out"], np.float32).reshape(T, D) for b in range(8)], 0)
    return out
```

```python
import math
from contextlib import ExitStack
import numpy as np
import ml_dtypes
import concourse.bass as bass
import concourse.mybir as mybir
from concourse.bass_utils import run_bass_kernel_spmd

F32 = mybir.dt.float32
BF16 = mybir.dt.bfloat16
ALU = mybir.AluOpType
AF = mybir.ActivationFunctionType
AX = mybir.AxisListType

T = 4096
D = 1024
INW = 4224
NT = 8
TT_ = 512
NEG_E = -math.exp(-0.5)
NORM_EPS = 1e-6
GN_EPS = 64e-5


class Buf:
    __slots__ = ('w', 'r', 'excl')

    def __init__(self):
        self.w = {}
        self.r = {}
        self.excl = False


class TL:
    __slots__ = ('t', 'b')

    def __init__(self, t):
        self.t = t
        self.b = Buf()

    def __getitem__(self, k):
        return self.t[k]


def _b(x):
    return x.b if isinstance(x, TL) else x


class Sched:
    ENG = ('tensor', 'vector', 'scalar', 'gpsimd', 'sync')

    def __init__(self, nc, sems, dma_sems):
        self.nc = nc
        self.sem = sems
        self.cnt = {e: 0 for e in self.ENG}
        self.ops = {e: [] for e in self.ENG}
        self.waited = {e: {} for e in self.ENG}
        self.dma_sems = dma_sems
        self.dma_cnt = [0] * len(dma_sems)
        self.dma_rr = 0
        self.dma_rr_q = {}

    def _wait(self, eng, key, val):
        if val <= 0 or self.waited[eng].get(key, 0) >= val:
            return
        self.waited[eng][key] = val
        sem = self.sem[key] if isinstance(key, str) else self.dma_sems[key]
        self.ops[eng].append(('wait', sem, val))

    def _deps(self, eng, reads, writes):
        for b in reads:
            for k, v in b.w.items():
                if k == eng and eng == 'tensor':
                    continue
                self._wait(eng, k, v)
            if b.excl:
                for k, v in b.r.items():
                    if k != eng:
                        self._wait(eng, k, v)
        for b in writes:
            for k, v in b.w.items():
                if k == eng and eng == 'tensor':
                    continue
                self._wait(eng, k, v)
            for k, v in b.r.items():
                if k == eng and eng == 'tensor':
                    continue
                self._wait(eng, k, v)

    def op(self, eng, fn, reads=(), writes=()):
        reads = [_b(x) for x in reads]
        writes = [_b(x) for x in writes]
        self._deps(eng, reads, writes)
        self.cnt[eng] += 1
        c = self.cnt[eng]
        self.ops[eng].append(('op', fn, self.sem[eng]))
        for b in reads:
            b.r[eng] = c
        for b in writes:
            b.w[eng] = c

    def dma(self, eng, fn, reads=(), writes=()):
        reads = [_b(x) for x in reads]
        writes = [_b(x) for x in writes]
        n = len(self.dma_sems) // 2
        base = 0 if eng == 'sync' else n
        rr = self.dma_rr_q.get(eng, 0)
        self.dma_rr_q[eng] = (rr + 1) % n
        i = base + rr
        self._wait(eng, i, self.dma_cnt[i])
        self._deps(eng, reads, writes)
        self.dma_cnt[i] += 16
        v = self.dma_cnt[i]
        self.ops[eng].append(('dma', fn, self.dma_sems[i]))
        for b in reads:
            b.r[i] = v
        for b in writes:
            b.w[i] = v

    def barrier(self):
        for e in self.ENG:
            for f in self.ENG:
                if f != e:
                    self._wait(e, f, self.cnt[f])
            for i in range(len(self.dma_sems)):
                self._wait(e, i, self.dma_cnt[i])

    def flush(self):
        nc = self.nc
        ops = self.ops

        def run(eng, lst):
            for o in lst:
                if o[0] == 'wait':
                    eng.wait_ge(o[1], o[2])
                elif o[0] == 'op':
                    o[1](eng).then_inc(o[2], 1)
                else:
                    o[1](eng).then_inc(o[2], 16)

        with nc.Block() as block:
            @block.tensor
            def _(e):
                run(e, ops['tensor'])

            @block.vector
            def _(e):
                run(e, ops['vector'])

            @block.scalar
            def _(e):
                run(e, ops['scalar'])

            @block.gpsimd
            def _(e):
                run(e, ops['gpsimd'])

            @block.sync
            def _(e):
                run(e, ops['sync'])
        self.ops = {e: [] for e in self.ENG}


class KB:
    def __init__(self, nc, S):
        self.nc = nc
        self.S = S
        self.rr = 0

    def mm(self, out, lhsT, rhs, start=True, stop=True, r=(), w=()):
        self.S.op('tensor', lambda e: e.matmul(out, lhsT=lhsT, rhs=rhs, start=start, stop=stop), r, w)

    def tr(self, out, in_, ident, r=(), w=()):
        self.S.op('tensor', lambda e: e.transpose(out=out, in_=in_, identity=ident), r, w)

    def act(self, out, in_, func, r=(), w=(), scale=1.0, bias=0.0, accum=None):
        if accum is None:
            self.S.op('scalar', lambda e: e.activation(out=out, in_=in_, func=func, bias=bias, scale=scale), r, w)
        else:
            self.S.op('scalar', lambda e: e.activation(out=out, in_=in_, func=func, bias=bias, scale=scale, accum_out=accum), r, w)

    def tt(self, out, in0, in1, op, r=(), w=(), eng='vector'):
        self.S.op(eng, lambda e: e.tensor_tensor(out=out, in0=in0, in1=in1, op=op), r, w)

    def ts(self, out, in0, s1, s2, op0, op1=None, r=(), w=(), eng='vector'):
        if op1 is None:
            self.S.op(eng, lambda e: e.tensor_scalar(out=out, in0=in0, scalar1=s1, scalar2=None, op0=op0), r, w)
        else:
            self.S.op(eng, lambda e: e.tensor_scalar(out=out, in0=in0, scalar1=s1, scalar2=s2, op0=op0, op1=op1), r, w)

    def stt(self, out, in0, scalar, in1, op0, op1, r=(), w=(), eng='vector'):
        self.S.op(eng, lambda e: e.scalar_tensor_tensor(out=out, in0=in0, scalar=scalar, in1=in1, op0=op0, op1=op1), r, w)

    def cp(self, out, in_, r=(), w=(), eng='vector'):
        if eng == 'scalar':
            self.act(out, in_, AF.Identity, r, w)
        else:
            self.S.op(eng, lambda e: e.tensor_copy(out=out, in_=in_), r, w)

    def cpa(self, out, in_, r=(), w=()):
        self.rr ^= 1
        self.cp(out, in_, r, w, eng='vector' if self.rr else 'scalar')

    def recip(self, out, in_, r=(), w=()):
        self.S.op('vector', lambda e: e.reciprocal(out=out, in_=in_), r, w)

    def red(self, out, in_, r=(), w=()):
        self.S.op('vector', lambda e: e.tensor_reduce(out=out, in_=in_, axis=AX.X, op=ALU.add), r, w)

    def memset(self, ap, val, w=(), eng='gpsimd'):
        self.S.op(eng, lambda e: e.memset(ap, val), (), w)

    def store(self, out, in_, r=(), w=()):
        self.S.dma('gpsimd', lambda e: e.dma_start(out=out, in_=in_), r, w)

    def dma(self, out, in_, r=(), w=(), eng='sync', slow=False):
        if slow:
            self.S.dma(eng, lambda e: e.dma_start(out=out, in_=in_, allow_slow_non_contiguous=True), r, w)
        else:
            self.S.dma(eng, lambda e: e.dma_start(out=out, in_=in_), r, w)


DBG = [99]
EXCL = [True]


def build_program(debug=False, phases=(0, 1, 2, 3), do_rwkv=True, n_tiles=NT):
    nc = bass.Bass("TRN2", target_bir_lowering=False)
    dk = "ExternalOutput" if debug else "Internal"

    def din(name, shape, dt=F32):
        return nc.dram_tensor(name, list(shape), dt, kind="ExternalInput").ap()

    x_d = din("x", [T, D])
    win_d = din("w_in", [D, INW])
    wout_d = din("w_out", [D, D])
    vecs_d = din("vecs", [128, 50])
    dupx_d = din("dupx", [65, 512])
    iup_d = din("iclr_up", [64, 512])
    gng_d = din("gn_g", [128, 512])
    gnb_d = din("gn_b", [128, 512])
    fg_d = din("fgain", [128, 1024])
    identb_d = din("ident_b", [128, 128], BF16)
    identf_d = din("ident_f", [128, 128])
    rswap_d = din("rswap", [128, 128], BF16)
    maskP_d = din("maskP", [128, 512], BF16)
    mask2_d = din("mask2", [128, 512])
    maskN_d = din("maskN", [128, 512])
    cm_d = din("cm", [128, 259])
    bones_d = din("bones", [128, 128])
    hsel_d = din("hsel", [128, 32])
    cos_d = din("cos_t", [16, T])
    sin_d = din("sin_t", [16, T])
    out_d = nc.dram_tensor("out", [T, D], F32, kind="ExternalOutput").ap()
    qT_d = nc.dram_tensor("qT_s", [512, T], BF16, kind=dk).ap()
    kT_d = nc.dram_tensor("kT_s", [512, T], BF16, kind=dk).ap()
    v_d = nc.dram_tensor("v_s", [T, 512], BF16, kind=dk).ap()
    gB_d = nc.dram_tensor("gB_s", [512, T], BF16, kind=dk).ap()
    yT_d = nc.dram_tensor("yT_s", [1024, T], BF16, kind=dk).ap()
    wsc_d = nc.dram_tensor("wsc_s", [33, 128, 1024], BF16, kind="Internal").ap()

    with ExitStack() as es:
        sems = {e: es.enter_context(nc.semaphore("s_" + e)) for e in Sched.ENG}
        dsems = [es.enter_context(nc.semaphore("d%d" % i)) for i in range(16)]
        S = Sched(nc, sems, dsems)
        K = KB(nc, S)

        def sb(ctx, name, shape, dt=F32):
            return TL(ctx.enter_context(nc.sbuf_tensor("sb_" + name, list(shape), dt)))

        def ps(ctx, name, shape, dt=F32):
            t = TL(ctx.enter_context(nc.psum_tensor("ps_" + name, list(shape), dt)))
            t.b.excl = EXCL[0]
            return t

        vecs = sb(es, "vecs", [128, 50])
        identb = sb(es, "identb", [128, 128], BF16)
        identf = sb(es, "identf", [128, 128])
        K.dma(vecs[:], vecs_d, w=[vecs])
        K.dma(identb[:], identb_d, w=[identb])
        K.dma(identf[:], identf_d, w=[identf])
        GAIN, MIX, OMM, IB, KNS, KIM, BON, NIB = 0, 8, 21, 34, 38, 42, 46, 34
        K.ts(vecs[:, OMM:OMM + 13], vecs[:, MIX:MIX + 13], -1.0, 1.0, ALU.mult, ALU.add, r=[vecs], w=[vecs])
        nib = sb(es, "nib", [128, 4])
        nkim = sb(es, "nkim", [128, 4])
        K.ts(nib[:], vecs[:, IB:IB + 4], -1.0, None, ALU.mult, r=[vecs], w=[nib])
        K.ts(nkim[:], vecs[:, KIM:KIM + 4], -1.0, None, ALU.mult, r=[vecs], w=[nkim])

        PB = [ps(es, "pb%d" % i, [128, 512]) for i in range(7)]
        PT = ps(es, "ptb", [128, 1024], BF16)

        if DBG[0] == 0:
            S.barrier()
            S.flush()
        if 1 in phases and DBG[0] >= 1:
            wscB = Buf()
            with ExitStack() as p0:
                stg = [sb(p0, "stg%d" % i, [128, 1408]) for i in range(2)]
                wo = [sb(p0, "wo%d" % i, [128, 1408], BF16) for i in range(2)]
                n = 0
                for kc in range(8):
                    for p3 in range(3):
                        st, wo_ = stg[n % 2], wo[n % 2]
                        K.dma(st[:], win_d[kc * 128:(kc + 1) * 128, p3 * 1408:(p3 + 1) * 1408], w=[st])
                        if n % 2 == 0:
                            K.ts(wo_[:], st[:], vecs[:, GAIN + kc:GAIN + kc + 1], None, ALU.mult, r=[st, vecs], w=[wo_])
                        else:
                            K.act(wo_[:], st[:], AF.Identity, r=[st, vecs], w=[wo_], scale=vecs[:, GAIN + kc:GAIN + kc + 1])
                        K.store(wsc_d[p3 * 11:(p3 + 1) * 11, :, kc * 128:(kc + 1) * 128].rearrange("j p c -> p j c"),
                                wo_[:].rearrange("p (j c) -> p j c", j=11), r=[wo_], w=[wscB])
                        n += 1
                S.barrier()
                S.flush()
            with ExitStack() as p1:
              if DBG[0] >= 2:
                phase1(nc, S, K, p1, sb, PB, PT, vecs, nib, nkim, identb, identf, wsc_d, wscB, x_d, dupx_d, iup_d, gng_d, gnb_d,
                       rswap_d, mask2_d, maskN_d, cm_d, bones_d, hsel_d, cos_d, sin_d, qT_d, kT_d, v_d, gB_d, yT_d,
                       do_rwkv, n_tiles)
                S.barrier()
                S.flush()
        with ExitStack() as p23:
            w3 = None
            if 3 in phases:
                woutb = sb(p23, "woutb", [128, 8 * 1024], BF16)
                fg = sb(p23, "fg", [128, 1024])
                stg3 = [sb(p23, "wstg%d" % i, [128, 1024]) for i in range(2)]
                K.dma(fg[:], fg_d, w=[fg])
                for kc in range(8):
                    st = stg3[kc % 2]
                    K.dma(st[:], wout_d[kc * 128:(kc + 1) * 128, :], w=[st])
                    K.cpa(woutb[:, kc * 1024:(kc + 1) * 1024], st[:], r=[st], w=[woutb])
                w3 = (woutb, fg)
            if 2 in phases:
                with ExitStack() as p2:
                    phase2(nc, S, K, p2, sb, PB, maskP_d, qT_d, kT_d, v_d, gB_d, yT_d)
                    S.barrier()
                    S.flush()
            if 3 in phases:
                with ExitStack() as p3:
                    phase3(nc, S, K, p3, sb, PB, w3, x_d, yT_d, out_d)
                    S.barrier()
                    S.flush()
    return nc


def phase1(nc, S, K, ctx, sb, PB, PT, vecs, nib, nkim, identb, identf, wsc_d, wscB, x_d, dupx_d, iup_d, gng_d, gnb_d,
           rswap_d, mask2_d, maskN_d, cm_d, bones_d, hsel_d, cos_d, sin_d, qT_d, kT_d, v_d, gB_d, yT_d,
           do_rwkv, n_tiles):
    GAIN, MIX, OMM, IB, KNS, KIM, BON = 0, 8, 21, 34, 38, 42, 46
    rswap = sb(ctx, "rswap", [128, 128], BF16)
    K.dma(rswap[:], rswap_d, w=[rswap])
    cosTs = [sb(ctx, "cosT%d" % i, [128, 512]) for i in range(2)]
    sinTs = [sb(ctx, "sinT%d" % i, [128, 512]) for i in range(2)]
    for i in range(2):
        K.memset(cosTs[i][:], 1.0, w=[cosTs[i]])
        K.memset(sinTs[i][:], 0.0, w=[sinTs[i]])
    if do_rwkv:
        dupx = sb(ctx, "dupx", [65, 512])
        iup = sb(ctx, "iup", [128, 512])
        gng = sb(ctx, "gng", [128, 512])
        gnb = sb(ctx, "gnb", [128, 512])
        mask2 = sb(ctx, "mask2", [128, 512])
        maskN = sb(ctx, "maskN", [128, 512])
        cm = sb(ctx, "cm", [128, 259])
        bones = sb(ctx, "bones", [128, 128])
        hsel = sb(ctx, "hsel", [128, 32])
        K.dma(dupx[:], dupx_d, w=[dupx])
        K.dma(iup[64:128, :], iup_d, w=[iup])
        K.dma(gng[:], gng_d, w=[gng])
        K.dma(gnb[:], gnb_d, w=[gnb])
        K.dma(mask2[:], mask2_d, w=[mask2])
        K.dma(maskN[:], maskN_d, w=[maskN])
        K.dma(cm[:], cm_d, w=[cm])
        K.dma(bones[:], bones_d, w=[bones])
        K.dma(hsel[:], hsel_d, w=[hsel])

    xt = [sb(ctx, "xt%d" % i, [128, 1024]) for i in range(2)]
    stat = [sb(ctx, "stat%d" % i, [128, 4]) for i in range(2)]
    hb = [sb(ctx, "hb%d" % i, [128, 1024], BF16) for i in range(2)]
    hTs = [sb(ctx, "hT%d" % i, [128, 8 * 512], BF16) for i in range(2)]
    tb = {}
    plast = sb(ctx, "plast", [128, 13])
    K.memset(plast[:], 0.0, w=[plast])
    qf = [sb(ctx, "qf0", [128, 512])] * 2
    q1 = [sb(ctx, "q1%d" % i, [128, 512], BF16) for i in range(2)]
    rtmp = sb(ctx, "rtmp", [128, 512])
    qb = [sb(ctx, "qb%d" % i, [128, 512], BF16) for i in range(2)]
    ge = [sb(ctx, "ge0", [128, 512])] * 2
    gb = [sb(ctx, "gb%d" % i, [128, 512], BF16) for i in range(2)]
    vtb = [sb(ctx, "vtb%d" % i, [128, 512], BF16) for i in range(2)]

    if do_rwkv:
        was = sb(ctx, "was", [128, 512])
        twx = sb(ctx, "twx", [65, 512])
        K.memset(twx[64:65, :], 1.0, w=[twx])
        sg = sb(ctx, "sg", [128, 4 * 512])
        rs = [sb(ctx, "rs0", [128, 512])] * 2
        ks = [sb(ctx, "ks0", [128, 512])] * 2
        vs = [sb(ctx, "vs0", [128, 512])] * 2
        E1 = sb(ctx, "E1", [128, 512])
        E2 = sb(ctx, "E2", [128, 512])
        E3 = sb(ctx, "E3", [128, 512])
        decs = sb(ctx, "decs", [128, 4 * 4 * 3])
        av = sb(ctx, "av", [128, 512])
        sv = sb(ctx, "sv", [128, 512])
        sq = sb(ctx, "sq", [128, 512])
        kp = sb(ctx, "kp", [128, 512])
        kkv = sv
        t1 = sq
        rkb = sq
        bv = av
        arT = [sb(ctx, "arT%d" % c, [128, 1024], BF16) for c in range(4)]
        kbT = [sb(ctx, "kbT%d" % c, [128, 1024], BF16) for c in range(4)]
        vbf = sb(ctx, "vbf", [128, 512], BF16)
        Ktok = sb(ctx, "Ktok", [128, 4 * 512], BF16)
        Btok = sb(ctx, "Btok", [128, 4 * 512], BF16)
        Vtok = sb(ctx, "Vtok", [128, 4 * 512], BF16)
        gA = sb(ctx, "gA", [128, 4 * 512], BF16)
        cbs = sb(ctx, "cbs", [128, 32])
        ABm = [sb(ctx, "ABm%d" % c, [128, 512], BF16) for c in range(4)]
        AKm = [sb(ctx, "AKm%d" % c, [128, 512], BF16) for c in range(4)]
        Nc = [[sb(ctx, "Nc%d_%d" % (g, i), [128, 512], BF16) for i in range(2)] for g in range(2)]
        Mc = [[sb(ctx, "Mc%d_%d" % (g, i), [128, 512], BF16) for i in range(2)] for g in range(2)]
        Pc = [[sb(ctx, "Pc%d_%d" % (g, i), [128, 512], BF16) for i in range(2)] for g in range(2)]
        St = sb(ctx, "St", [128, 256])
        K.memset(St[:], 0.0, w=[St])
        St0m = sb(ctx, "St0m", [128, 256], BF16)
        Xb = sb(ctx, "Xb", [128, 512], BF16)
        Ub = sb(ctx, "Ub", [128, 512], BF16)
        yv = sb(ctx, "yv", [128, 512])
        ysq = sb(ctx, "ysq", [128, 512])
        gst = sb(ctx, "gst", [128, 64])
        stmp = sb(ctx, "stmp", [128, 256])
        stmp2 = sb(ctx, "stmp2", [128, 256])
        yaT = [sb(ctx, "yaT%d" % i, [128, 512], BF16) for i in range(2)]

    nx = [0]

    def shift_evac(pb, j, dst):
        K.act(dst[:], pb[:], AF.Identity, r=[pb, vecs], w=[dst], scale=vecs[:, OMM + j:OMM + j + 1])
        K.stt(dst[:, 1:512], pb[:, 0:511], vecs[:, MIX + j:MIX + j + 1], dst[:, 1:512], ALU.mult, ALU.add, r=[pb, vecs, dst], w=[dst])
        K.stt(dst[:, 0:1], plast[:, j:j + 1], vecs[:, MIX + j:MIX + j + 1], dst[:, 0:1], ALU.mult, ALU.add, r=[plast, vecs, dst], w=[dst])
        K.cp(plast[:, j:j + 1], pb[:, 511:512], r=[pb], w=[plast])

    pj = [0]
    wch = [sb(ctx, "wch%d" % i, [128, 1024], BF16) for i in range(4)]
    wv = sb(ctx, "wv", [128, 4096], BF16)
    for cc in range(4):
        K.dma(wv[:, cc * 1024:(cc + 1) * 1024], wsc_d[25 + cc], r=[wscB], w=[wv])

    def proj(j):
        wb = wch[pj[0] % 4]
        pb = PB[pj[0] % 2]
        pj[0] += 1
        K.dma(wb[:], wsc_d[j], r=[wscB], w=[wb])
        if DBG[0] == 39:
            return pb
        for kc in range(8):
            K.mm(pb[:], wb[:, kc * 128:(kc + 1) * 128], tb['hT'][:, kc * 512:(kc + 1) * 512],
                 start=(kc == 0), stop=(kc == 7), r=[wb, tb['hT']], w=[pb])
        return pb

    def sigmoid_from_exp(dst, r=(), w=()):
        K.act(dst, dst, AF.Ln, r=r, w=w, bias=1.0)
        K.act(dst, dst, AF.Exp, r=r, w=w, scale=-1.0)

    PTf = PT[:].bitcast(F32)

    def q_gen(t0, hT, cosT, sinT):
        def projq(j):
            wb = wch[pj[0] % 4]
            pj[0] += 1
            pb = PB[6]
            K.dma(wb[:], wsc_d[j], r=[wscB], w=[wb])
            for kc in range(8):
                K.mm(pb[:], wb[:, kc * 128:(kc + 1) * 128], hT[:, kc * 512:(kc + 1) * 512],
                     start=(kc == 0), stop=(kc == 7), r=[wb, hT], w=[pb])
            return pb
        nq = 0
        for j, dst_d in [(17 + c, qT_d) for c in range(4)] + [(21 + c, kT_d) for c in range(4)]:
            c = (j - 17) % 4
            i2 = nq % 2
            nq += 1
            pb = projq(j)
            K.cp(qf[0][:], pb[:], r=[pb], w=[qf[0]], eng='scalar')
            K.cp(q1[i2][:], pb[:], r=[pb], w=[q1[i2]], eng='scalar')
            K.mm(PTf, rswap[:], q1[i2][:], r=[rswap, q1[i2]], w=[PT])
            K.tt(rtmp[:], PTf, sinT[:], ALU.mult, r=[PT, sinT], w=[rtmp])
            K.tt(qf[0][:], qf[0][:], cosT[:], ALU.mult, r=[qf[0], cosT], w=[qf[0]])
            K.tt(qb[i2][:], qf[0][:], rtmp[:], ALU.add, r=[qf[0], rtmp], w=[qb[i2]])
            K.store(dst_d[c * 128:(c + 1) * 128, t0:t0 + 512], qb[i2][:], r=[qb[i2]])
            yield
        for c in range(4):
            i2 = c % 2
            pb = projq(29 + c)
            K.act(ge[0][:], pb[:], AF.Exp, r=[pb], w=[ge[0]], scale=-1.0)
            sigmoid_from_exp(ge[0][:], r=[ge[0]], w=[ge[0]])
            K.tt(gb[i2][:], ge[0][:], pb[:], ALU.mult, r=[ge[0], pb], w=[gb[i2]])
            K.store(gB_d[c * 128:(c + 1) * 128, t0:t0 + 512], gb[i2][:], r=[gb[i2]])
            yield
        for tc in range(4):
            pb = PB[6]
            for cc in range(4):
                for kc in range(8):
                    K.mm(pb[:, cc * 128:(cc + 1) * 128], hT[:, kc * 512 + tc * 128: kc * 512 + (tc + 1) * 128],
                         wv[:, cc * 1024 + kc * 128: cc * 1024 + (kc + 1) * 128],
                         start=(kc == 0), stop=(kc == 7), r=[wv, hT], w=[pb])
            i2 = tc % 2
            K.cpa(vtb[i2][:], pb[:], r=[pb], w=[vtb[i2]])
            K.store(v_d[t0 + tc * 128: t0 + (tc + 1) * 128, :], vtb[i2][:], r=[vtb[i2]])
            yield

    qg = [None]

    def q_step(n=1):
        for _ in range(n):
            if qg[0] is None:
                return
            try:
                next(qg[0])
            except StopIteration:
                qg[0] = None
                return

    gn = [None]

    def gn_step(n=1):
        for _ in range(n):
            if gn[0] is None:
                return
            try:
                next(gn[0])
            except StopIteration:
                gn[0] = None
                return

    def gn_gen(tc, t0):
        y3 = yv[:].rearrange("p (h v) -> p h v", h=8)
        K.red(gst[:, 0:8], y3, r=[yv], w=[gst])
        K.act(ysq[:], yv[:], AF.Square, r=[yv], w=[ysq])
        K.red(gst[:, 8:16], ysq[:].rearrange("p (h v) -> p h v", h=8), r=[ysq], w=[gst])
        yield
        K.ts(gst[:, 16:24], gst[:, 0:8], 1.0 / 64, None, ALU.mult, r=[gst], w=[gst])
        K.tt(gst[:, 24:32], gst[:, 16:24], gst[:, 16:24], ALU.mult, r=[gst], w=[gst])
        K.stt(gst[:, 32:40], gst[:, 8:16], 1.0 / 64, gst[:, 24:32], ALU.mult, ALU.subtract, r=[gst], w=[gst])
        K.act(gst[:, 40:48], gst[:, 32:40], AF.Ln, r=[gst], w=[gst], bias=GN_EPS)
        K.act(gst[:, 48:56], gst[:, 40:48], AF.Exp, r=[gst], w=[gst], scale=-0.5)
        yield
        K.tt(y3, y3, gst[:, 16:24].unsqueeze(2).to_broadcast([128, 8, 64]), ALU.subtract, r=[yv, gst], w=[yv])
        K.tt(y3, y3, gst[:, 48:56].unsqueeze(2).to_broadcast([128, 8, 64]), ALU.mult, r=[yv, gst], w=[yv])
        yield
        K.tt(yv[:], yv[:], gng[:], ALU.mult, r=[yv, gng], w=[yv])
        K.tt(yv[:], yv[:], gnb[:], ALU.add, r=[yv, gnb], w=[yv])
        yield
        K.tt(ysq[:].rearrange("p (h v) -> p h v", h=8), Vtok[:, tc * 512:(tc + 1) * 512].rearrange("p (h v) -> p h v", h=8),
             cbs[:, tc * 8:(tc + 1) * 8].unsqueeze(2).to_broadcast([128, 8, 64]), ALU.mult, r=[Vtok, cbs], w=[ysq])
        K.tt(yv[:], yv[:], ysq[:], ALU.add, r=[yv, ysq], w=[yv])
        yield
        for c in range(4):
            K.tr(PTf[:, c * 128:(c + 1) * 128], yv[:, c * 128:(c + 1) * 128], identf[:], r=[yv, identf], w=[PT])
        yo_ = yaT[tc % 2]
        K.tt(yo_[:].rearrange("p (c t) -> p c t", c=4), PTf.rearrange("p (c t) -> p c t", c=4),
             gA[:].rearrange("p (c t) -> p c t", c=4)[:, :, tc * 128:(tc + 1) * 128], ALU.mult, r=[PT, gA], w=[yo_])
        K.store(yT_d[0:512, t0 + tc * 128: t0 + (tc + 1) * 128].rearrange("(c p) t -> p c t", p=128),
                yo_[:].rearrange("p (c t) -> p c t", c=4), r=[yo_])
        yield

    def head_gen(ti):
        t0 = ti * 512
        hT = hTs[ti % 2]
        cosT, sinT = cosTs[ti % 2], sinTs[ti % 2]
        for sub in range(4):
            i = nx[0] % 2
            nx[0] += 1
            xx = xt[i]
            stt_ = stat[i]
            K.dma(xx[:], x_d[t0 + sub * 128: t0 + (sub + 1) * 128, :], w=[xx])
            K.memset(stt_[:], 0.0, w=[stt_])
            K.act(hb[i][:], xx[:], AF.Square, r=[xx], w=[hb[i], stt_], accum=stt_[:, 0:1])
            K.act(stt_[:, 1:2], stt_[:, 0:1], AF.Ln, r=[stt_], w=[stt_], scale=1.0 / D, bias=NORM_EPS)
            K.act(stt_[:, 2:3], stt_[:, 1:2], AF.Exp, r=[stt_], w=[stt_], scale=-0.5)
            K.act(hb[i][:], xx[:], AF.Identity, r=[xx, stt_], w=[hb[i]], scale=stt_[:, 2:3])
            for kc in range(8):
                K.tr(PT[:, kc * 128:(kc + 1) * 128], hb[i][:, kc * 128:(kc + 1) * 128], identb[:], r=[hb[i], identb], w=[PT])
            K.cpa(hT[:].rearrange("p (k t) -> p k t", k=8)[:, :, sub * 128:(sub + 1) * 128],
                  PT[:].rearrange("p (k t) -> p k t", k=8), r=[PT], w=[hT])
            yield
        for base in (0, 64):
            K.dma(cosT[base:base + 16, :], cos_d[:, t0:t0 + 512], w=[cosT])
            K.dma(sinT[base:base + 16, :], sin_d[:, t0:t0 + 512], w=[sinT])
        if do_rwkv:
            wb = wch[pj[0] % 4]
            pj[0] += 1
            pb = PB[6]
            K.dma(wb[:], wsc_d[12], r=[wscB], w=[wb])
            for kc in range(8):
                K.mm(pb[:], wb[:, kc * 128:(kc + 1) * 128], hT[:, kc * 512:(kc + 1) * 512],
                     start=(kc == 0), stop=(kc == 7), r=[wb, hT], w=[pb])
            shift_evac(pb, 12, was)
            K.act(twx[0:64, :], was[0:64, :], AF.Exp, r=[was], w=[twx], scale=2.0)
            K.act(twx[0:64, :], twx[0:64, :], AF.Ln, r=[twx], w=[twx], bias=1.0)
            K.act(twx[0:64, :], twx[0:64, :], AF.Exp, r=[twx], w=[twx], scale=-1.0)
            K.ts(twx[0:64, :], twx[0:64, :], -2.0, 1.0, ALU.mult, ALU.add, r=[twx], w=[twx])
            yield
            for tc in range(4):
                pz = PB[6]
                K.mm(pz[:], twx[0:65, tc * 128:(tc + 1) * 128], dupx[0:65, :], r=[twx, dupx], w=[pz])
                dst = sg[:, tc * 512:(tc + 1) * 512]
                K.act(dst, pz[:], AF.Exp, r=[pz], w=[sg], scale=-1.0)
                sigmoid_from_exp(dst, r=[sg], w=[sg])
                yield

    hd = [head_gen(0)]

    def hd_step(n=1):
        for _ in range(n):
            if hd[0] is None:
                return
            try:
                next(hd[0])
            except StopIteration:
                hd[0] = None
                return

    for ti in range(n_tiles):
        t0 = ti * 512
        hd_step(1000)
        tb['hT'] = hTs[ti % 2]
        qg[0] = q_gen(t0, hTs[ti % 2], cosTs[ti % 2], sinTs[ti % 2])
        if ti + 1 < n_tiles:
            hd[0] = head_gen(ti + 1)
        if do_rwkv:
            if DBG[0] == 50:
                return
            for c in range(4):
                i2 = c % 2
                pb = proj(c)
                shift_evac(pb, c, rs[i2])
                pb = proj(4 + c)
                shift_evac(pb, 4 + c, ks[i2])
                pb = proj(8 + c)
                shift_evac(pb, 8 + c, vs[i2])
                r_, k_, v_ = rs[i2], ks[i2], vs[i2]
                pza = PB[2]
                K.mm(pza[:], iup[64:128, c * 128:(c + 1) * 128], was[64:128, :], r=[iup, was], w=[pza])
                K.act(av[:], pza[:], AF.Exp, r=[pza, nib], w=[av], scale=-1.0, bias=nib[:, c:c + 1])
                sigmoid_from_exp(av[:], r=[av], w=[av])
                q_step(1)
                if DBG[0] == 51:
                    continue
                for tc in range(4):
                    pcs = PB[3 + tc % 2]
                    K.mm(pcs[:, 0:259], sg[:, tc * 512 + c * 128: tc * 512 + (c + 1) * 128], cm[:, 0:259], r=[sg, cm], w=[pcs])
                    K.act(E1[:, tc * 128:(tc + 1) * 128], pcs[:, 0:128], AF.Exp, r=[pcs], w=[E1])
                    K.act(E2[:, tc * 128:(tc + 1) * 128], pcs[:, 128:256], AF.Exp, r=[pcs], w=[E2])
                    K.act(E3[:, tc * 128:(tc + 1) * 128], pcs[:, 0:128], AF.Exp, r=[pcs], w=[E3], scale=-1.0)
                    o = (c * 4 + tc) * 3
                    K.act(decs[:, o:o + 3], pcs[:, 256:259], AF.Exp, r=[pcs], w=[decs])
                if DBG[0] == 52:
                    continue
                K.act(sv[:], k_[:], AF.Identity, r=[k_, vecs], w=[sv], scale=vecs[:, KNS + c:KNS + c + 1])
                K.act(sq[:], sv[:], AF.Square, r=[sv], w=[sq])
                pss = PB[2]
                K.mm(pss[:], bones[:], sq[:], r=[bones, sq], w=[pss])
                K.act(sq[:], pss[:], AF.Ln, r=[pss], w=[sq], bias=1e-30)
                K.act(sq[:], sq[:], AF.Exp, r=[sq], w=[sq], scale=-0.5)
                K.tt(kkv[:], sv[:], sq[:], ALU.mult, r=[sv, sq], w=[kkv])
                q_step(1)
                K.act(t1[:], av[:], AF.Identity, r=[av, vecs, nkim], w=[t1], scale=vecs[:, KIM + c:KIM + c + 1], bias=nkim[:, c:c + 1])
                K.stt(kp[:], t1[:], 1.0, k_[:], ALU.add, ALU.mult, r=[t1, k_], w=[kp])
                K.tt(bv[:], kkv[:], av[:], ALU.mult, r=[kkv, av], w=[bv])
                K.stt(rkb[:], r_[:], vecs[:, BON + c:BON + c + 1], kp[:], ALU.mult, ALU.mult, r=[r_, vecs, kp], w=[rkb])
                pbon = PB[5]
                for tc in range(4):
                    K.mm(pbon[:, tc * 8 + 2 * c: tc * 8 + 2 * c + 2], rkb[:, tc * 128:(tc + 1) * 128], hsel[:, c * 8 + 2 * c: c * 8 + 2 * c + 2],
                         r=[rkb, hsel], w=[pbon])
                if DBG[0] == 53:
                    continue
                a4 = arT[c][:].rearrange("p (tc w t) -> p tc w t", tc=4, w=2)
                k4 = kbT[c][:].rearrange("p (tc w t) -> p tc w t", tc=4, w=2)
                v3 = lambda tl: tl[:].rearrange("p (tc t) -> p tc t", tc=4)
                K.stt(a4[:, :, 0, :], v3(kkv), -1.0, v3(E2), ALU.mult, ALU.mult, r=[kkv, E2], w=[arT[c]])
                K.tt(a4[:, :, 1, :], v3(r_), v3(E1), ALU.mult, r=[r_, E1], w=[arT[c]])
                K.tt(k4[:, :, 0, :], v3(kp), v3(E3), ALU.mult, r=[kp, E3], w=[kbT[c]])
                K.tt(k4[:, :, 1, :], v3(bv), v3(E3), ALU.mult, r=[bv, E3], w=[kbT[c]])
                K.cp(vbf[:], v_[:], r=[v_], w=[vbf], eng='scalar')
                for which, dstT in ((0, Ktok), (1, Btok), (2, Vtok)):
                    half = (c * 3 + which) % 2
                    pt = PT[:, half * 512:(half + 1) * 512]
                    for tc in range(4):
                        if which < 2:
                            src = kbT[c][:, tc * 256 + which * 128: tc * 256 + (which + 1) * 128]
                            rr = [kbT[c], identb]
                        else:
                            src = vbf[:, tc * 128:(tc + 1) * 128]
                            rr = [vbf, identb]
                        K.tr(pt[:, tc * 128:(tc + 1) * 128], src, identb[:], r=rr, w=[PT])
                    K.cpa(dstT[:].rearrange("p (tc ch) -> p tc ch", tc=4)[:, :, c * 128:(c + 1) * 128],
                          pt.rearrange("p (tc ch) -> p tc ch", tc=4), r=[PT], w=[dstT])
                pb = proj(13 + c)
                gdst = gA[:, c * 512:(c + 1) * 512]
                K.act(kp[:], pb[:], AF.Exp, r=[pb], w=[kp], scale=-1.0)
                sigmoid_from_exp(kp[:], r=[kp], w=[kp])
                K.tt(gdst, kp[:], pb[:], ALU.mult, r=[kp, pb], w=[gA])
                q_step(1)
            if DBG[0] in (51, 52, 53):
                return
            K.cp(cbs[:], PB[5][:, 0:32], r=[PB[5]], w=[cbs])

            if DBG[0] == 54:
                return
            for tc in range(4):
                for c in range(4):
                    pH = [PB[(c % 2) * 2], PB[(c % 2) * 2 + 1]]
                    for half in range(2):
                        hs = slice(half * 64, half * 64 + 64)
                        ar = arT[c][hs, tc * 256:(tc + 1) * 256]
                        K.mm(pH[half][:, 0:256], kbT[c][hs, tc * 256 + 128: tc * 256 + 256], ar, r=[kbT[c], arT[c]], w=[pH[half]])
                        K.mm(pH[half][:, 256:512], kbT[c][hs, tc * 256: tc * 256 + 128], ar, r=[kbT[c], arT[c]], w=[pH[half]])
                    for half in range(2):
                        K.tt(ABm[c][:, half * 256:(half + 1) * 256], pH[half][:, 0:256], mask2[:, 0:256], ALU.mult, r=[pH[half], mask2], w=[ABm[c]])
                        K.tt(AKm[c][:, half * 256:(half + 1) * 256], pH[half][:, 256:512], mask2[:, 0:256], ALU.mult, r=[pH[half], mask2], w=[AKm[c]])
                if DBG[0] in (55, 551, 552):
                    continue
                q_step(1)
                gn_step(1)
                for g in range(2):
                    pNh = [PB[4], PB[5]]
                    for hh in range(4):
                        h = g * 4 + hh
                        c, half = h // 2, h % 2
                        hs = slice(half * 64, half * 64 + 64)
                        col = g * 256 + (hh // 2) * 128
                        K.mm(pNh[half][:, col:col + 128], arT[c][hs, tc * 256: tc * 256 + 128], kbT[c][hs, tc * 256 + 128: tc * 256 + 256],
                             r=[arT[c], kbT[c]], w=[pNh[half]])
                    for half in range(2):
                        K.tt(Nc[g][0][:].rearrange("p (pr h t) -> p pr h t", pr=2, h=2)[:, :, half, :],
                             pNh[half][:, g * 256:(g + 1) * 256].rearrange("p (pr t) -> p pr t", pr=2),
                             maskN[:, 0:256].rearrange("p (pr t) -> p pr t", pr=2), ALU.mult, r=[pNh[half], maskN], w=[Nc[g][0]])
                    for pr in range(2):
                        c = g * 2 + pr
                        src = ABm[c][:].rearrange("p (h w t) -> p h w t", h=2, w=2)[:, :, 0, :]
                        K.tt(Pc[g][0][:, pr * 256:(pr + 1) * 256].rearrange("p (h t) -> p h t", h=2), src,
                             identb[:].unsqueeze(1).to_broadcast([128, 2, 128]), ALU.add, r=[ABm[c], identb], w=[Pc[g][0]])
                if DBG[0] == 56:
                    continue
                def Mop(g, j, hh):
                    if j == 0:
                        h = g * 4 + hh
                        c, half = h // 2, h % 2
                        return ABm[c][:, half * 256: half * 256 + 128], ABm[c]
                    return Mc[g][j % 2][:, hh * 128:(hh + 1) * 128], Mc[g][j % 2]

                for j in range(6):
                    q_step(1)
                    gn_step(1)
                    if tc >= 1:
                        hd_step(1)
                    cur, nxt = j % 2, (j + 1) % 2
                    pNn = [PB[0], PB[3]]
                    pMn = [PB[1], PB[4]]
                    pP = [PB[2], PB[5]]
                    for g in range(2):
                        for hh in range(4):
                            sl = slice(hh * 128, (hh + 1) * 128)
                            m_ap, m_tl = Mop(g, j, hh)
                            K.mm(pNn[g][:, sl], m_ap, Nc[g][cur][:, sl], r=[m_tl, Nc[g][cur]], w=[pNn[g]])
                        if j < 5:
                            for hh in range(4):
                                sl = slice(hh * 128, (hh + 1) * 128)
                                m_ap, m_tl = Mop(g, j, hh)
                                K.mm(pMn[g][:, sl], Nc[g][cur][:, sl], m_ap, r=[m_tl, Nc[g][cur]], w=[pMn[g]])
                    for g in range(2):
                        K.cp(Nc[g][nxt][:], pNn[g][:], r=[pNn[g]], w=[Nc[g][nxt]], eng='scalar')
                        if j < 5:
                            K.cp(Mc[g][nxt][:], pMn[g][:], r=[pMn[g]], w=[Mc[g][nxt]], eng=('vector' if g == 0 else 'scalar'))
                    for g in range(2):
                        for hh in range(4):
                            sl = slice(hh * 128, (hh + 1) * 128)
                            K.mm(pP[g][:, sl], identb[:], Pc[g][cur][:, sl], start=True, stop=False, r=[identb, Pc[g][cur]], w=[pP[g]])
                            K.mm(pP[g][:, sl], Nc[g][nxt][:, sl], Pc[g][cur][:, sl], start=False, stop=True, r=[Nc[g][nxt], Pc[g][cur]], w=[pP[g]])
                    K.cp(Pc[0][nxt][:], pP[0][:], r=[pP[0]], w=[Pc[0][nxt]], eng='scalar')
                    K.cp(Pc[1][nxt][:], pP[1][:], r=[pP[1]], w=[Pc[1][nxt]], eng='vector')
                if DBG[0] == 57:
                    continue
                TTt = [Pc[0][0], Pc[1][0]]
                gn_step(1000)
                d4 = decs[:].rearrange("p (c tc k) -> p c tc k", c=4, tc=4)
                St3 = St[:].rearrange("p (c v) -> p c v", c=4)
                K.tt(St0m[:].rearrange("p (c v) -> p c v", c=4), St3, d4[:, :, tc, 0:1].to_broadcast([128, 4, 64]), ALU.mult,
                     r=[St, decs], w=[St0m])
                pX, pU, pY, pS = PB[0], PB[1], PB[2], PB[3]
                for h in range(8):
                    c, half = h // 2, h % 2
                    hs = slice(half * 64, half * 64 + 64)
                    vh = Vtok[:, tc * 512 + h * 64: tc * 512 + (h + 1) * 64]
                    K.mm(pX[:, h * 64:(h + 1) * 64], arT[c][hs, tc * 256: tc * 256 + 128], St0m[hs, c * 64:(c + 1) * 64],
                         start=True, stop=False, r=[arT[c], St0m], w=[pX])
                    K.mm(pX[:, h * 64:(h + 1) * 64], AKm[c][:, half * 256: half * 256 + 128], vh,
                         start=False, stop=True, r=[AKm[c], Vtok], w=[pX])
                K.cp(Xb[:], pX[:], r=[pX], w=[Xb], eng='scalar')
                q_step(1)
                for h in range(8):
                    g, hh = h // 4, h % 4
                    K.mm(pU[:, h * 64:(h + 1) * 64], TTt[g][:, hh * 128:(hh + 1) * 128], Xb[:, h * 64:(h + 1) * 64], r=[TTt[g], Xb], w=[pU])
                K.cp(Ub[:], pU[:], r=[pU], w=[Ub], eng='vector')
                q_step(1)
                for h in range(8):
                    c, half = h // 2, h % 2
                    hs = slice(half * 64, half * 64 + 64)
                    vh = Vtok[:, tc * 512 + h * 64: tc * 512 + (h + 1) * 64]
                    uh = Ub[:, h * 64:(h + 1) * 64]
                    yo = pY[:, h * 64:(h + 1) * 64]
                    K.mm(yo, arT[c][hs, tc * 256 + 128: tc * 256 + 256], St0m[hs, c * 64:(c + 1) * 64], start=True, stop=False, r=[arT[c], St0m], w=[pY])
                    K.mm(yo, ABm[c][:, half * 256 + 128: half * 256 + 256], uh, start=False, stop=False, r=[ABm[c], Ub], w=[pY])
                    K.mm(yo, AKm[c][:, half * 256 + 128: half * 256 + 256], vh, start=False, stop=True, r=[AKm[c], Vtok], w=[pY])
                    so = pS[hs, c * 64:(c + 1) * 64]
                    K.mm(so, Btok[:, tc * 512 + h * 64: tc * 512 + (h + 1) * 64], uh, start=True, stop=False, r=[Btok, Ub], w=[pS])
                    K.mm(so, Ktok[:, tc * 512 + h * 64: tc * 512 + (h + 1) * 64], vh, start=False, stop=True, r=[Ktok, Vtok], w=[pS])
                K.tt(stmp[:].rearrange("p (c v) -> p c v", c=4), pS[:, 0:256].rearrange("p (c v) -> p c v", c=4),
                     d4[:, :, tc, 2:3].to_broadcast([128, 4, 64]), ALU.mult, r=[pS, decs], w=[stmp])
                K.tt(stmp2[:].rearrange("p (c v) -> p c v", c=4), St3, d4[:, :, tc, 1:2].to_broadcast([128, 4, 64]), ALU.mult,
                     r=[St, decs], w=[stmp2])
                K.tt(St[:], stmp[:], stmp2[:], ALU.add, r=[stmp, stmp2], w=[St])
                if DBG[0] == 58:
                    continue
                q_step(1)
                K.cp(yv[:], pY[:], r=[pY], w=[yv], eng='scalar')
                gn[0] = gn_gen(tc, t0)
            gn_step(1000)

        if 50 <= DBG[0] <= 59 or DBG[0] in (551, 552):
            return
        q_step(1000)


def phase2(nc, S, K, ctx, sb, PB, maskP_d, qT_d, kT_d, v_d, gB_d, yT_d):
    maskP = sb(ctx, "maskP", [128, 512], BF16)
    K.dma(maskP[:], maskP_d, w=[maskP])
    qp = [sb(ctx, "qp%d" % i, [128, T], BF16) for i in range(2)]
    kp = [sb(ctx, "kpp%d" % i, [128, T], BF16) for i in range(2)]
    gp = [sb(ctx, "gp%d" % i, [128, T], BF16) for i in range(2)]
    Vd = [[sb(ctx, "Vd%d_%d" % (i, d), [128, 32 * 128], BF16) for d in range(3)] for i in range(2)]
    for i in range(2):
        for d in range(3):
            K.memset(Vd[i][d][:].rearrange("p (b c) -> p b c", c=128)[:, :, (1 - i) * 64:(1 - i) * 64 + 64], 1.0, w=[Vd[i][d]])
    acc = sb(ctx, "acc", [128, T])
    Pt = [sb(ctx, "Pt%d" % i, [128, 512], BF16) for i in range(4)]
    rden = sb(ctx, "rden", [128, T])
    yb = [sb(ctx, "ybo%d" % i, [128, T], BF16) for i in range(2)]
    DIL = (1, 4, 16)

    def loads(h):
        c, half = h // 2, h % 2
        pi = c % 2
        if half == 0:
            K.dma(qp[pi][:], qT_d[c * 128:(c + 1) * 128, :], w=[qp[pi]])
            K.dma(kp[pi][:], kT_d[c * 128:(c + 1) * 128, :], w=[kp[pi]])
            K.dma(gp[pi][:], gB_d[c * 128:(c + 1) * 128, :], w=[gp[pi]])
        vi = h % 2
        for di, d in enumerate(DIL):
            nb = 32 // d
            V3 = Vd[vi][di][:].rearrange("p (b c) -> p b c", c=128)
            for r in range(d):
                src = v_d[:, h * 64:(h + 1) * 64].rearrange("(n j r) c -> r j n c", r=d, j=128)[r]
                K.dma(V3[:, r * nb:(r + 1) * nb, half * 64:half * 64 + 64], src, w=[Vd[vi][di]])

    SB5 = [PB[0], PB[1], PB[2], PB[3], PB[6]]
    for pbk in SB5:
        K.memset(pbk[:], 0.0, w=[pbk], eng='vector')
    jobs = []
    for h in range(8):
        for di, d in enumerate(DIL):
            nb = 32 // d
            blocks = [(r, n) for n in range(nb) for r in range(d)] if d > 1 else [(0, n) for n in range(32)]
            for gi in range(8):
                grp = blocks[gi * 4:(gi + 1) * 4]
                for pair in range(2):
                    jobs.append(dict(h=h, di=di, d=d, nb=nb, gi=gi, pair=pair, grp=grp))
    for k, jb in enumerate(jobs):
        jb['k'] = k
        jb['first'] = (k == 0 or jobs[k - 1]['h'] != jb['h'])
        jb['last'] = (k == len(jobs) - 1 or jobs[k + 1]['h'] != jb['h'])

    def stage_S(jb):
        h, d, k = jb['h'], jb['d'], jb['k']
        c, half = h // 2, h % 2
        hs = slice(half * 64, half * 64 + 64)
        pi = c % 2
        q_, k_ = qp[pi], kp[pi]
        pS = SB5[k % 5]
        for bi in range(2):
            r, n = jb['grp'][jb['pair'] * 2 + bi]
            qs = q_[hs, r + d * 128 * n: r + d * 128 * n + d * 127 + 1: d]
            if n > 0:
                kprev = k_[hs, r + d * 128 * (n - 1): r + d * 128 * (n - 1) + d * 127 + 1: d]
                K.mm(pS[:, bi * 256: bi * 256 + 128], kprev, qs, r=[k_, q_], w=[pS])
            kcur = k_[hs, r + d * 128 * n: r + d * 128 * n + d * 127 + 1: d]
            K.mm(pS[:, bi * 256 + 128: bi * 256 + 256], kcur, qs, r=[k_, q_], w=[pS])

    def stage_E(jb):
        k = jb['k']
        pS = SB5[k % 5]
        P_ = Pt[k % 4]
        grp, pair = jb['grp'], jb['pair']
        K.act(P_[:], pS[:], AF.Exp, r=[pS], w=[P_], scale=0.125)
        K.tt(P_[:], P_[:], maskP[:], ALU.mult, r=[P_, maskP], w=[P_])

    def stage_V(jb):
        h, d, k, di, nb, gi = jb['h'], jb['d'], jb['k'], jb['di'], jb['nb'], jb['gi']
        c, half = h // 2, h % 2
        hs = slice(half * 64, half * 64 + 64)
        pi = c % 2
        vi = h % 2
        V3 = Vd[vi][di][:].rearrange("p (b c) -> p b c", c=128)
        P_ = Pt[k % 4]
        po = PB[4 + gi % 2]
        grp, pair = jb['grp'], jb['pair']
        for bi in range(2):
            r, n = grp[pair * 2 + bi]
            slot = pair * 2 + bi
            oo = po[:, slot * 128:(slot + 1) * 128]
            blk = r * nb + n
            if n > 0:
                K.mm(oo, V3[:, blk - 1, :], P_[:, bi * 256: bi * 256 + 128], start=True, stop=False, r=[Vd[vi][di], P_], w=[po])
            K.mm(oo, V3[:, blk, :], P_[:, bi * 256 + 128: bi * 256 + 256], start=(n == 0), stop=True, r=[Vd[vi][di], P_], w=[po])

    def stage_A(jb):
        h, d, k, di, nb, gi = jb['h'], jb['d'], jb['k'], jb['di'], jb['nb'], jb['gi']
        c, half = h // 2, h % 2
        hs = slice(half * 64, half * 64 + 64)
        pi = c % 2
        po = PB[4 + gi % 2]
        grp, pair = jb['grp'], jb['pair']
        if pair == 1:
            if d == 1:
                K.cp(acc[:, gi * 512:(gi + 1) * 512], po[:], r=[po], w=[acc], eng='scalar')
            else:
                r0, n = grp[0]
                span = d * 128
                av = acc[:, n * span:(n + 1) * span].rearrange("p (i r) -> p r i", r=d)[:, r0:r0 + 4, :]
                K.tt(av, av, po[:].rearrange("p (r i) -> p r i", r=4), ALU.add, r=[acc, po], w=[acc])
        if jb['last']:
            os_ = slice((1 - half) * 64, (1 - half) * 64 + 64)
            K.recip(rden[hs, :], acc[os_, :], r=[acc], w=[rden])
            K.tt(acc[hs, :], acc[hs, :], rden[hs, :], ALU.mult, r=[acc, rden], w=[acc])
            K.tt(yb[pi][hs, :], acc[hs, :], gp[pi][hs, :], ALU.mult, r=[acc, gp[pi]], w=[yb[pi]])
            if half == 1:
                K.store(yT_d[512 + c * 128: 512 + (c + 1) * 128, :], yb[pi][:], r=[yb[pi]])
            if h + 2 < 8:
                loads(h + 2)

    DEPTH = 3
    LAG = 2
    nj = len(jobs)
    loads(0)
    loads(1)
    for k in range(min(DEPTH, nj)):
        stage_S(jobs[k])
    for k in range(nj):
        if k + DEPTH < nj:
            stage_S(jobs[k + DEPTH])
        stage_E(jobs[k])
        stage_V(jobs[k])
        if k - LAG >= 0:
            stage_A(jobs[k - LAG])
    for k in range(max(0, nj - LAG), nj):
        stage_A(jobs[k])


def phase3(nc, S, K, ctx, sb, PB, w3, x_d, yT_d, out_d):
    woutb, fg = w3
    yt = [sb(ctx, "yt%d" % i, [128, 8 * 128], BF16) for i in range(3)]
    xr = [sb(ctx, "xr%d" % i, [128, 1024]) for i in range(3)]
    xo = [sb(ctx, "xo%d" % i, [128, 1024]) for i in range(2)]
    oo = [sb(ctx, "oo%d" % i, [128, 1024]) for i in range(2)]
    stat = [sb(ctx, "stat3_%d" % i, [128, 4]) for i in range(2)]
    for t in range(32):
        i3, i2 = t % 3, t % 2
        K.dma(yt[i3][:].rearrange("p (c t) -> p c t", c=8), yT_d[:, t * 128:(t + 1) * 128].rearrange("(c p) t -> p c t", p=128), w=[yt[i3]])
        K.dma(xr[i3][:], x_d[t * 128:(t + 1) * 128, :], w=[xr[i3]])
        for hf in range(2):
            pb = PB[(t * 2 + hf) % 4]
            for kc in range(8):
                K.mm(pb[:], yt[i3][:, kc * 128:(kc + 1) * 128], woutb[:, kc * 1024 + hf * 512: kc * 1024 + (hf + 1) * 512],
                     start=(kc == 0), stop=(kc == 7), r=[yt[i3], woutb], w=[pb])
            K.tt(xo[i2][:, hf * 512:(hf + 1) * 512], pb[:], xr[i3][:, hf * 512:(hf + 1) * 512], ALU.add, r=[pb, xr[i3]], w=[xo[i2]])
        st = stat[i2]
        K.memset(st[:], 0.0, w=[st])
        K.act(oo[i2][:], xo[i2][:], AF.Square, r=[xo[i2]], w=[oo[i2], st], accum=st[:, 0:1])
        K.act(st[:, 1:2], st[:, 0:1], AF.Ln, r=[st], w=[st], scale=1.0 / D, bias=NORM_EPS)
        K.act(st[:, 2:3], st[:, 1:2], AF.Exp, r=[st], w=[st], scale=-0.5)
        K.stt(oo[i2][:], xo[i2][:], st[:, 2:3], fg[:], ALU.mult, ALU.mult, r=[xo[i2], st, fg], w=[oo[i2]])
        K.store(out_d[t * 128:(t + 1) * 128, :], oo[i2][:], r=[oo[i2]])


def host_consts():
    bf = ml_dtypes.bfloat16
    c = {}
    c["ident_b"] = np.eye(128, dtype=np.float32).astype(bf)
    c["ident_f"] = np.eye(128, dtype=np.float32)
    rs = np.zeros((128, 128), np.float32)
    for p in range(128):
        m = p % 64
        if m < 8:
            rs[p + 8, p] = -1.0
        elif m < 16:
            rs[p - 8, p] = 1.0
    c["rswap"] = rs.astype(bf)
    j = np.arange(128)[:, None]
    i = np.arange(128)[None, :]
    U = (j >= i).astype(np.float32)
    L = (j <= i).astype(np.float32)
    c["maskP"] = np.concatenate([U, L, U, L], 1).astype(bf)
    SU = (j < i).astype(np.float32)
    UI = (j <= i).astype(np.float32)
    c["mask2"] = np.concatenate([SU, UI, SU, UI], 1)
    SL = (j > i).astype(np.float32)
    c["maskN"] = np.concatenate([SL, SL, SL, SL], 1)
    s = np.arange(128)[:, None]
    t = np.arange(128)[None, :]
    mid = (s <= 63).astype(np.float64)
    cm = np.zeros((128, 259), np.float64)
    cm[:, 0:128] = (s <= t) - mid
    cm[:, 128:256] = (s < t) - mid
    cm[:, 256:257] = mid
    cm[:, 257] = 1.0
    cm[:, 258:259] = 1.0 - mid
    c["cm"] = (cm * NEG_E).astype(np.float32)
    p = np.arange(128)
    c["bones"] = (p[:, None] // 64 == p[None, :] // 64).astype(np.float32)
    hsel = np.zeros((128, 4, 8), np.float32)
    for cc in range(4):
        for pp in range(128):
            hsel[pp, cc, 2 * cc + pp // 64] = 1.0
    c["hsel"] = hsel.reshape(128, 32)
    half = 8
    inv = (np.float32(500000.0) ** (-(np.arange(half, dtype=np.float32) * np.float32(2.0) / np.float32(16)))).astype(np.float32)
    ang = (np.arange(T, dtype=np.float32)[None, :] * inv[:, None]).astype(np.float32)
    cs = np.cos(ang).astype(np.float32)
    sn = np.sin(ang).astype(np.float32)
    c["cos_t"] = np.concatenate([cs, cs], 0)
    c["sin_t"] = np.concatenate([sn, sn], 0)
    return c


def pc(v, nchunk):
    return np.ascontiguousarray(np.asarray(v, np.float32).reshape(nchunk, 128).T)


def make_in_maps(inputs):
    c = host_consts()
    vecs = np.zeros((128, 50), np.float32)
    vecs[:, 0:8] = pc(inputs["norm_gain"][0], 8)
    vecs[:, 8:21] = pc(inputs["shift_mix"][0], 13)
    vecs[:, 34:38] = pc(inputs["iclr_base"][0], 4)
    vecs[:, 38:42] = pc(inputs["key_norm_scale"][0], 4)
    vecs[:, 42:46] = pc(inputs["key_iclr_mix"][0], 4)
    vecs[:, 46:50] = pc(inputs["bonus"][0].reshape(-1), 4)
    shared = dict(c)
    shared["w_in"] = np.ascontiguousarray(inputs["w_in"][0], np.float32)
    shared["w_out"] = np.ascontiguousarray(inputs["w_out"][0], np.float32)
    shared["vecs"] = vecs
    shared["dupx"] = np.ascontiguousarray(np.concatenate([inputs["decay_up"][0], inputs["decay_base"][0][None, :]], 0), np.float32)
    shared["iclr_up"] = np.ascontiguousarray(inputs["iclr_up"][0], np.float32)
    shared["gn_g"] = np.ascontiguousarray(np.broadcast_to(inputs["gn_gain"][0][None, :], (128, 512)), np.float32)
    shared["gn_b"] = np.ascontiguousarray(np.broadcast_to(inputs["gn_bias"][0][None, :], (128, 512)), np.float32)
    shared["fgain"] = np.ascontiguousarray(np.broadcast_to(np.asarray(inputs["final_gain"])[None, :], (128, 1024)), np.float32)
    maps = []
    for b in range(8):
        m = dict(shared)
        m["x"] = np.ascontiguousarray(inputs["x"][b], np.float32)
        maps.append(m)
    return maps


def kernel(**inputs):
    inputs = {k: np.asarray(v) for k, v in inputs.items()}
    nc = build_program()
    maps = make_in_maps(inputs)
    res = run_bass_kernel_spmd(nc, maps, core_ids=list(range(8)))
    out = np.stack([np.asarray(res.results[b]["out"], np.float32).reshape(T, D) for b in range(8)], 0)
    return out
```

```python
import math
from contextlib import ExitStack
import numpy as np
import ml_dtypes
import concourse.bass as bass
import concourse.mybir as mybir
from concourse.bass_utils import run_bass_kernel_spmd

F32 = mybir.dt.float32
BF16 = mybir.dt.bfloat16
ALU = mybir.AluOpType
AF = mybir.ActivationFunctionType
AX = mybir.AxisListType

T = 4096
D = 1024
INW = 4224
NT = 8
TT_ = 512
NEG_E = -math.exp(-0.5)
NORM_EPS = 1e-6
GN_EPS = 64e-5


class Buf:
    __slots__ = ('w', 'r', 'excl')

    def __init__(self):
        self.w = {}
        self.r = {}
        self.excl = False


class TL:
    __slots__ = ('t', 'b')

    def __init__(self, t):
        self.t = t
        self.b = Buf()

    def __getitem__(self, k):
        return self.t[k]


def _b(x):
    return x.b if isinstance(x, TL) else x


class Sched:
    ENG = ('tensor', 'vector', 'scalar', 'gpsimd', 'sync')

    def __init__(self, nc, sems, dma_sems):
        self.nc = nc
        self.sem = sems
        self.cnt = {e: 0 for e in self.ENG}
        self.ops = {e: [] for e in self.ENG}
        self.waited = {e: {} for e in self.ENG}
        self.dma_sems = dma_sems
        self.dma_cnt = [0] * len(dma_sems)
        self.dma_rr = 0
        self.dma_rr_q = {}

    def _wait(self, eng, key, val):
        if val <= 0 or self.waited[eng].get(key, 0) >= val:
            return
        self.waited[eng][key] = val
        sem = self.sem[key] if isinstance(key, str) else self.dma_sems[key]
        self.ops[eng].append(('wait', sem, val))

    def _deps(self, eng, reads, writes):
        for b in reads:
            for k, v in b.w.items():
                if k == eng and eng == 'tensor':
                    continue
                self._wait(eng, k, v)
            if b.excl:
                for k, v in b.r.items():
                    if k != eng:
                        self._wait(eng, k, v)
        for b in writes:
            for k, v in b.w.items():
                if k == eng and eng == 'tensor':
                    continue
                self._wait(eng, k, v)
            for k, v in b.r.items():
                if k == eng and eng == 'tensor':
                    continue
                self._wait(eng, k, v)

    def op(self, eng, fn, reads=(), writes=()):
        reads = [_b(x) for x in reads]
        writes = [_b(x) for x in writes]
        self._deps(eng, reads, writes)
        self.cnt[eng] += 1
        c = self.cnt[eng]
        self.ops[eng].append(('op', fn, self.sem[eng]))
        for b in reads:
            b.r[eng] = c
        for b in writes:
            b.w[eng] = c

    def dma(self, eng, fn, reads=(), writes=()):
        reads = [_b(x) for x in reads]
        writes = [_b(x) for x in writes]
        n = len(self.dma_sems) // 2
        base = 0 if eng == 'sync' else n
        rr = self.dma_rr_q.get(eng, 0)
        self.dma_rr_q[eng] = (rr + 1) % n
        i = base + rr
        self._wait(eng, i, self.dma_cnt[i])
        self._deps(eng, reads, writes)
        self.dma_cnt[i] += 16
        v = self.dma_cnt[i]
        self.ops[eng].append(('dma', fn, self.dma_sems[i]))
        for b in reads:
            b.r[i] = v
        for b in writes:
            b.w[i] = v

    def barrier(self):
        for e in self.ENG:
            for f in self.ENG:
                if f != e:
                    self._wait(e, f, self.cnt[f])
            for i in range(len(self.dma_sems)):
                self._wait(e, i, self.dma_cnt[i])

    def flush(self):
        nc = self.nc
        ops = self.ops

        def run(eng, lst):
            for o in lst:
                if o[0] == 'wait':
                    eng.wait_ge(o[1], o[2])
                elif o[0] == 'op':
                    o[1](eng).then_inc(o[2], 1)
                else:
                    o[1](eng).then_inc(o[2], 16)

        with nc.Block() as block:
            @block.tensor
            def _(e):
                run(e, ops['tensor'])

            @block.vector
            def _(e):
                run(e, ops['vector'])

            @block.scalar
            def _(e):
                run(e, ops['scalar'])

            @block.gpsimd
            def _(e):
                run(e, ops['gpsimd'])

            @block.sync
            def _(e):
                run(e, ops['sync'])
        self.ops = {e: [] for e in self.ENG}


class KB:
    def __init__(self, nc, S):
        self.nc = nc
        self.S = S
        self.rr = 0

    def mm(self, out, lhsT, rhs, start=True, stop=True, r=(), w=()):
        self.S.op('tensor', lambda e: e.matmul(out, lhsT=lhsT, rhs=rhs, start=start, stop=stop), r, w)

    def tr(self, out, in_, ident, r=(), w=()):
        self.S.op('tensor', lambda e: e.transpose(out=out, in_=in_, identity=ident), r, w)

    def act(self, out, in_, func, r=(), w=(), scale=1.0, bias=0.0, accum=None):
        if accum is None:
            self.S.op('scalar', lambda e: e.activation(out=out, in_=in_, func=func, bias=bias, scale=scale), r, w)
        else:
            self.S.op('scalar', lambda e: e.activation(out=out, in_=in_, func=func, bias=bias, scale=scale, accum_out=accum), r, w)

    def tt(self, out, in0, in1, op, r=(), w=(), eng='vector'):
        self.S.op(eng, lambda e: e.tensor_tensor(out=out, in0=in0, in1=in1, op=op), r, w)

    def ts(self, out, in0, s1, s2, op0, op1=None, r=(), w=(), eng='vector'):
        if op1 is None:
            self.S.op(eng, lambda e: e.tensor_scalar(out=out, in0=in0, scalar1=s1, scalar2=None, op0=op0), r, w)
        else:
            self.S.op(eng, lambda e: e.tensor_scalar(out=out, in0=in0, scalar1=s1, scalar2=s2, op0=op0, op1=op1), r, w)

    def stt(self, out, in0, scalar, in1, op0, op1, r=(), w=(), eng='vector'):
        self.S.op(eng, lambda e: e.scalar_tensor_tensor(out=out, in0=in0, scalar=scalar, in1=in1, op0=op0, op1=op1), r, w)

    def cp(self, out, in_, r=(), w=(), eng='vector'):
        if eng == 'scalar':
            self.act(out, in_, AF.Identity, r, w)
        else:
            self.S.op(eng, lambda e: e.tensor_copy(out=out, in_=in_), r, w)

    def cpa(self, out, in_, r=(), w=()):
        self.rr ^= 1
        self.cp(out, in_, r, w, eng='vector' if self.rr else 'scalar')

    def recip(self, out, in_, r=(), w=()):
        self.S.op('vector', lambda e: e.reciprocal(out=out, in_=in_), r, w)

    def red(self, out, in_, r=(), w=()):
        self.S.op('vector', lambda e: e.tensor_reduce(out=out, in_=in_, axis=AX.X, op=ALU.add), r, w)

    def memset(self, ap, val, w=(), eng='gpsimd'):
        self.S.op(eng, lambda e: e.memset(ap, val), (), w)

    def store(self, out, in_, r=(), w=()):
        self.S.dma('gpsimd', lambda e: e.dma_start(out=out, in_=in_), r, w)

    def dma(self, out, in_, r=(), w=(), eng='sync', slow=False):
        if slow:
            self.S.dma(eng, lambda e: e.dma_start(out=out, in_=in_, allow_slow_non_contiguous=True), r, w)
        else:
            self.S.dma(eng, lambda e: e.dma_start(out=out, in_=in_), r, w)


DBG = [99]
EXCL = [True]


def build_program(debug=False, phases=(0, 1, 2, 3), do_rwkv=True, n_tiles=NT):
    nc = bass.Bass("TRN2", target_bir_lowering=False)
    dk = "ExternalOutput" if debug else "Internal"

    def din(name, shape, dt=F32):
        return nc.dram_tensor(name, list(shape), dt, kind="ExternalInput").ap()

    x_d = din("x", [T, D])
    win_d = din("w_in", [D, INW])
    wout_d = din("w_out", [D, D])
    vecs_d = din("vecs", [128, 50])
    dupx_d = din("dupx", [65, 512])
    iup_d = din("iclr_up", [64, 512])
    gng_d = din("gn_g", [128, 512])
    gnb_d = din("gn_b", [128, 512])
    fg_d = din("fgain", [128, 1024])
    identb_d = din("ident_b", [128, 128], BF16)
    identf_d = din("ident_f", [128, 128])
    rswap_d = din("rswap", [128, 128], BF16)
    maskP_d = din("maskP", [128, 512], BF16)
    mask2_d = din("mask2", [128, 512])
    maskN_d = din("maskN", [128, 512])
    cm_d = din("cm", [128, 259])
    bones_d = din("bones", [128, 128])
    hsel_d = din("hsel", [128, 32])
    cos_d = din("cos_t", [16, T])
    sin_d = din("sin_t", [16, T])
    out_d = nc.dram_tensor("out", [T, D], F32, kind="ExternalOutput").ap()
    qT_d = nc.dram_tensor("qT_s", [512, T], BF16, kind=dk).ap()
    kT_d = nc.dram_tensor("kT_s", [512, T], BF16, kind=dk).ap()
    v_d = nc.dram_tensor("v_s", [T, 512], BF16, kind=dk).ap()
    gB_d = nc.dram_tensor("gB_s", [512, T], BF16, kind=dk).ap()
    yT_d = nc.dram_tensor("yT_s", [1024, T], BF16, kind=dk).ap()
    wsc_d = nc.dram_tensor("wsc_s", [33, 128, 1024], BF16, kind="Internal").ap()

    with ExitStack() as es:
        sems = {e: es.enter_context(nc.semaphore("s_" + e)) for e in Sched.ENG}
        dsems = [es.enter_context(nc.semaphore("d%d" % i)) for i in range(16)]
        S = Sched(nc, sems, dsems)
        K = KB(nc, S)

        def sb(ctx, name, shape, dt=F32):
            return TL(ctx.enter_context(nc.sbuf_tensor("sb_" + name, list(shape), dt)))

        def ps(ctx, name, shape, dt=F32):
            t = TL(ctx.enter_context(nc.psum_tensor("ps_" + name, list(shape), dt)))
            t.b.excl = EXCL[0]
            return t

        vecs = sb(es, "vecs", [128, 50])
        identb = sb(es, "identb", [128, 128], BF16)
        identf = sb(es, "identf", [128, 128])
        K.dma(vecs[:], vecs_d, w=[vecs])
        K.dma(identb[:], identb_d, w=[identb])
        K.dma(identf[:], identf_d, w=[identf])
        GAIN, MIX, OMM, IB, KNS, KIM, BON, NIB = 0, 8, 21, 34, 38, 42, 46, 34
        K.ts(vecs[:, OMM:OMM + 13], vecs[:, MIX:MIX + 13], -1.0, 1.0, ALU.mult, ALU.add, r=[vecs], w=[vecs])
        nib = sb(es, "nib", [128, 4])
        nkim = sb(es, "nkim", [128, 4])
        K.ts(nib[:], vecs[:, IB:IB + 4], -1.0, None, ALU.mult, r=[vecs], w=[nib])
        K.ts(nkim[:], vecs[:, KIM:KIM + 4], -1.0, None, ALU.mult, r=[vecs], w=[nkim])

        PB = [ps(es, "pb%d" % i, [128, 512]) for i in range(7)]
        PT = ps(es, "ptb", [128, 1024], BF16)

        if DBG[0] == 0:
            S.barrier()
            S.flush()
        if 1 in phases and DBG[0] >= 1:
            wscB = Buf()
            with ExitStack() as p0:
                stg = [sb(p0, "stg%d" % i, [128, 1408]) for i in range(4)]
                wo = [sb(p0, "wo%d" % i, [128, 1408], BF16) for i in range(6)]
                n = 0
                for kc in range(8):
                    for p3 in range(3):
                        st, wo_ = stg[n % 4], wo[n % 6]
                        K.dma(st[:], win_d[kc * 128:(kc + 1) * 128, p3 * 1408:(p3 + 1) * 1408], w=[st])
                        if n % 2 == 0:
                            K.ts(wo_[:], st[:], vecs[:, GAIN + kc:GAIN + kc + 1], None, ALU.mult, r=[st, vecs], w=[wo_])
                        else:
                            K.act(wo_[:], st[:], AF.Identity, r=[st, vecs], w=[wo_], scale=vecs[:, GAIN + kc:GAIN + kc + 1])
                        K.store(wsc_d[p3 * 11:(p3 + 1) * 11, :, kc * 128:(kc + 1) * 128].rearrange("j p c -> p j c"),
                                wo_[:].rearrange("p (j c) -> p j c", j=11), r=[wo_], w=[wscB])
                        n += 1
                S.barrier()
                S.flush()
            with ExitStack() as p1:
              if DBG[0] >= 2:
                phase1(nc, S, K, p1, sb, PB, PT, vecs, nib, nkim, identb, identf, wsc_d, wscB, x_d, dupx_d, iup_d, gng_d, gnb_d,
                       rswap_d, mask2_d, maskN_d, cm_d, bones_d, hsel_d, cos_d, sin_d, qT_d, kT_d, v_d, gB_d, yT_d,
                       do_rwkv, n_tiles)
                S.barrier()
                S.flush()
        with ExitStack() as p23:
            w3 = None
            if 3 in phases:
                woutb = sb(p23, "woutb", [128, 8 * 1024], BF16)
                fg = sb(p23, "fg", [128, 1024])
                stg3 = [sb(p23, "wstg%d" % i, [128, 1024]) for i in range(2)]
                K.dma(fg[:], fg_d, w=[fg])
                for kc in range(8):
                    st = stg3[kc % 2]
                    K.dma(st[:], wout_d[kc * 128:(kc + 1) * 128, :], w=[st])
                    K.cpa(woutb[:, kc * 1024:(kc + 1) * 1024], st[:], r=[st], w=[woutb])
                w3 = (woutb, fg)
            if 2 in phases:
                with ExitStack() as p2:
                    phase2(nc, S, K, p2, sb, PB, maskP_d, qT_d, kT_d, v_d, gB_d, yT_d)
                    S.barrier()
                    S.flush()
            if 3 in phases:
                with ExitStack() as p3:
                    phase3(nc, S, K, p3, sb, PB, w3, x_d, yT_d, out_d)
                    S.barrier()
                    S.flush()
    return nc


def phase1(nc, S, K, ctx, sb, PB, PT, vecs, nib, nkim, identb, identf, wsc_d, wscB, x_d, dupx_d, iup_d, gng_d, gnb_d,
           rswap_d, mask2_d, maskN_d, cm_d, bones_d, hsel_d, cos_d, sin_d, qT_d, kT_d, v_d, gB_d, yT_d,
           do_rwkv, n_tiles):
    GAIN, MIX, OMM, IB, KNS, KIM, BON = 0, 8, 21, 34, 38, 42, 46
    rswap = sb(ctx, "rswap", [128, 128], BF16)
    K.dma(rswap[:], rswap_d, w=[rswap])
    cosTs = [sb(ctx, "cosT%d" % i, [128, 512]) for i in range(2)]
    sinTs = [sb(ctx, "sinT%d" % i, [128, 512]) for i in range(2)]
    for i in range(2):
        K.memset(cosTs[i][:], 1.0, w=[cosTs[i]])
        K.memset(sinTs[i][:], 0.0, w=[sinTs[i]])
    if do_rwkv:
        dupx = sb(ctx, "dupx", [65, 512])
        iup = sb(ctx, "iup", [128, 512])
        gng = sb(ctx, "gng", [128, 512])
        gnb = sb(ctx, "gnb", [128, 512])
        mask2 = sb(ctx, "mask2", [128, 512])
        maskN = sb(ctx, "maskN", [128, 512])
        cm = sb(ctx, "cm", [128, 259])
        bones = sb(ctx, "bones", [128, 128])
        hsel = sb(ctx, "hsel", [128, 32])
        K.dma(dupx[:], dupx_d, w=[dupx])
        K.dma(iup[64:128, :], iup_d, w=[iup])
        K.dma(gng[:], gng_d, w=[gng])
        K.dma(gnb[:], gnb_d, w=[gnb])
        K.dma(mask2[:], mask2_d, w=[mask2])
        K.dma(maskN[:], maskN_d, w=[maskN])
        K.dma(cm[:], cm_d, w=[cm])
        K.dma(bones[:], bones_d, w=[bones])
        K.dma(hsel[:], hsel_d, w=[hsel])

    xt = [sb(ctx, "xt%d" % i, [128, 1024]) for i in range(2)]
    stat = [sb(ctx, "stat%d" % i, [128, 4]) for i in range(2)]
    hb = [sb(ctx, "hb%d" % i, [128, 1024], BF16) for i in range(2)]
    hTs = [sb(ctx, "hT%d" % i, [128, 8 * 512], BF16) for i in range(2)]
    tb = {}
    plast = sb(ctx, "plast", [128, 13])
    K.memset(plast[:], 0.0, w=[plast])
    qf = [sb(ctx, "qf0", [128, 512])] * 2
    q1 = [sb(ctx, "q1%d" % i, [128, 512], BF16) for i in range(2)]
    rtmp = sb(ctx, "rtmp", [128, 512])
    qb = [sb(ctx, "qb%d" % i, [128, 512], BF16) for i in range(2)]
    ge = [sb(ctx, "ge0", [128, 512])] * 2
    gb = [sb(ctx, "gb%d" % i, [128, 512], BF16) for i in range(2)]
    vtb = [sb(ctx, "vtb%d" % i, [128, 512], BF16) for i in range(2)]

    if do_rwkv:
        was = sb(ctx, "was", [128, 512])
        twx = sb(ctx, "twx", [65, 512])
        K.memset(twx[64:65, :], 1.0, w=[twx])
        sg = sb(ctx, "sg", [128, 4 * 512])
        rs = [sb(ctx, "rs0", [128, 512])] * 2
        ks = [sb(ctx, "ks0", [128, 512])] * 2
        vs = [sb(ctx, "vs0", [128, 512])] * 2
        E1 = sb(ctx, "E1", [128, 512])
        E2 = sb(ctx, "E2", [128, 512])
        E3 = sb(ctx, "E3", [128, 512])
        decs = sb(ctx, "decs", [128, 4 * 4 * 3])
        av = sb(ctx, "av", [128, 512])
        sv = sb(ctx, "sv", [128, 512])
        sq = sb(ctx, "sq", [128, 512])
        kp = sb(ctx, "kp", [128, 512])
        kkv = sv
        t1 = sq
        rkb = sq
        bv = av
        arT = [sb(ctx, "arT%d" % c, [128, 1024], BF16) for c in range(4)]
        kbT = [sb(ctx, "kbT%d" % c, [128, 1024], BF16) for c in range(4)]
        vbf = sb(ctx, "vbf", [128, 512], BF16)
        Ktok = sb(ctx, "Ktok", [128, 4 * 512], BF16)
        Btok = sb(ctx, "Btok", [128, 4 * 512], BF16)
        Vtok = sb(ctx, "Vtok", [128, 4 * 512], BF16)
        gA = sb(ctx, "gA", [128, 4 * 512], BF16)
        cbs = sb(ctx, "cbs", [128, 32])
        ABm = [sb(ctx, "ABm%d" % c, [128, 512], BF16) for c in range(4)]
        AKm = [sb(ctx, "AKm%d" % c, [128, 512], BF16) for c in range(4)]
        Nc = [[sb(ctx, "Nc%d_%d" % (g, i), [128, 512], BF16) for i in range(2)] for g in range(2)]
        Mc = [[sb(ctx, "Mc%d_%d" % (g, i), [128, 512], BF16) for i in range(2)] for g in range(2)]
        Pc = [[sb(ctx, "Pc%d_%d" % (g, i), [128, 512], BF16) for i in range(2)] for g in range(2)]
        St = sb(ctx, "St", [128, 256])
        K.memset(St[:], 0.0, w=[St])
        St0m = sb(ctx, "St0m", [128, 256], BF16)
        Xb = sb(ctx, "Xb", [128, 512], BF16)
        Ub = sb(ctx, "Ub", [128, 512], BF16)
        yv = sb(ctx, "yv", [128, 512])
        ysq = sb(ctx, "ysq", [128, 512])
        gst = sb(ctx, "gst", [128, 64])
        stmp = sb(ctx, "stmp", [128, 256])
        stmp2 = sb(ctx, "stmp2", [128, 256])
        yaT = [sb(ctx, "yaT%d" % i, [128, 512], BF16) for i in range(2)]

    nx = [0]

    def shift_evac(pb, j, dst):
        K.act(dst[:], pb[:], AF.Identity, r=[pb, vecs], w=[dst], scale=vecs[:, OMM + j:OMM + j + 1])
        K.stt(dst[:, 1:512], pb[:, 0:511], vecs[:, MIX + j:MIX + j + 1], dst[:, 1:512], ALU.mult, ALU.add, r=[pb, vecs, dst], w=[dst])
        K.stt(dst[:, 0:1], plast[:, j:j + 1], vecs[:, MIX + j:MIX + j + 1], dst[:, 0:1], ALU.mult, ALU.add, r=[plast, vecs, dst], w=[dst])
        K.cp(plast[:, j:j + 1], pb[:, 511:512], r=[pb], w=[plast])

    pj = [0]
    wch = [sb(ctx, "wch%d" % i, [128, 1024], BF16) for i in range(4)]
    wv = sb(ctx, "wv", [128, 4096], BF16)
    for cc in range(4):
        K.dma(wv[:, cc * 1024:(cc + 1) * 1024], wsc_d[25 + cc], r=[wscB], w=[wv])

    def proj(j):
        wb = wch[pj[0] % 4]
        pb = PB[pj[0] % 2]
        pj[0] += 1
        K.dma(wb[:], wsc_d[j], r=[wscB], w=[wb])
        if DBG[0] == 39:
            return pb
        for kc in range(8):
            K.mm(pb[:], wb[:, kc * 128:(kc + 1) * 128], tb['hT'][:, kc * 512:(kc + 1) * 512],
                 start=(kc == 0), stop=(kc == 7), r=[wb, tb['hT']], w=[pb])
        return pb

    def sigmoid_from_exp(dst, r=(), w=()):
        K.act(dst, dst, AF.Ln, r=r, w=w, bias=1.0)
        K.act(dst, dst, AF.Exp, r=r, w=w, scale=-1.0)

    PTf = PT[:].bitcast(F32)

    def q_gen(t0, hT, cosT, sinT):
        def projq(j):
            wb = wch[pj[0] % 4]
            pj[0] += 1
            pb = PB[6]
            K.dma(wb[:], wsc_d[j], r=[wscB], w=[wb])
            for kc in range(8):
                K.mm(pb[:], wb[:, kc * 128:(kc + 1) * 128], hT[:, kc * 512:(kc + 1) * 512],
                     start=(kc == 0), stop=(kc == 7), r=[wb, hT], w=[pb])
            return pb
        nq = 0
        for j, dst_d in [(17 + c, qT_d) for c in range(4)] + [(21 + c, kT_d) for c in range(4)]:
            c = (j - 17) % 4
            i2 = nq % 2
            nq += 1
            pb = projq(j)
            K.cp(qf[0][:], pb[:], r=[pb], w=[qf[0]], eng='scalar')
            K.cp(q1[i2][:], pb[:], r=[pb], w=[q1[i2]], eng='scalar')
            K.mm(PTf, rswap[:], q1[i2][:], r=[rswap, q1[i2]], w=[PT])
            K.tt(rtmp[:], PTf, sinT[:], ALU.mult, r=[PT, sinT], w=[rtmp])
            K.tt(qf[0][:], qf[0][:], cosT[:], ALU.mult, r=[qf[0], cosT], w=[qf[0]])
            K.tt(qb[i2][:], qf[0][:], rtmp[:], ALU.add, r=[qf[0], rtmp], w=[qb[i2]])
            K.store(dst_d[c * 128:(c + 1) * 128, t0:t0 + 512], qb[i2][:], r=[qb[i2]])
            yield
        for c in range(4):
            i2 = c % 2
            pb = projq(29 + c)
            K.act(ge[0][:], pb[:], AF.Exp, r=[pb], w=[ge[0]], scale=-1.0)
            sigmoid_from_exp(ge[0][:], r=[ge[0]], w=[ge[0]])
            K.tt(gb[i2][:], ge[0][:], pb[:], ALU.mult, r=[ge[0], pb], w=[gb[i2]])
            K.store(gB_d[c * 128:(c + 1) * 128, t0:t0 + 512], gb[i2][:], r=[gb[i2]])
            yield
        for tc in range(4):
            pb = PB[6]
            for cc in range(4):
                for kc in range(8):
                    K.mm(pb[:, cc * 128:(cc + 1) * 128], hT[:, kc * 512 + tc * 128: kc * 512 + (tc + 1) * 128],
                         wv[:, cc * 1024 + kc * 128: cc * 1024 + (kc + 1) * 128],
                         start=(kc == 0), stop=(kc == 7), r=[wv, hT], w=[pb])
            i2 = tc % 2
            K.cpa(vtb[i2][:], pb[:], r=[pb], w=[vtb[i2]])
            K.store(v_d[t0 + tc * 128: t0 + (tc + 1) * 128, :], vtb[i2][:], r=[vtb[i2]])
            yield

    qg = [None]

    def q_step(n=1):
        for _ in range(n):
            if qg[0] is None:
                return
            try:
                next(qg[0])
            except StopIteration:
                qg[0] = None
                return

    gn = [None]

    def gn_step(n=1):
        for _ in range(n):
            if gn[0] is None:
                return
            try:
                next(gn[0])
            except StopIteration:
                gn[0] = None
                return

    def gn_gen(tc, t0):
        y3 = yv[:].rearrange("p (h v) -> p h v", h=8)
        K.red(gst[:, 0:8], y3, r=[yv], w=[gst])
        K.act(ysq[:], yv[:], AF.Square, r=[yv], w=[ysq])
        K.red(gst[:, 8:16], ysq[:].rearrange("p (h v) -> p h v", h=8), r=[ysq], w=[gst])
        yield
        K.ts(gst[:, 16:24], gst[:, 0:8], 1.0 / 64, None, ALU.mult, r=[gst], w=[gst])
        K.tt(gst[:, 24:32], gst[:, 16:24], gst[:, 16:24], ALU.mult, r=[gst], w=[gst])
        K.stt(gst[:, 32:40], gst[:, 8:16], 1.0 / 64, gst[:, 24:32], ALU.mult, ALU.subtract, r=[gst], w=[gst])
        K.act(gst[:, 40:48], gst[:, 32:40], AF.Ln, r=[gst], w=[gst], bias=GN_EPS)
        K.act(gst[:, 48:56], gst[:, 40:48], AF.Exp, r=[gst], w=[gst], scale=-0.5)
        yield
        K.tt(y3, y3, gst[:, 16:24].unsqueeze(2).to_broadcast([128, 8, 64]), ALU.subtract, r=[yv, gst], w=[yv])
        K.tt(y3, y3, gst[:, 48:56].unsqueeze(2).to_broadcast([128, 8, 64]), ALU.mult, r=[yv, gst], w=[yv])
        yield
        K.tt(yv[:], yv[:], gng[:], ALU.mult, r=[yv, gng], w=[yv])
        K.tt(yv[:], yv[:], gnb[:], ALU.add, r=[yv, gnb], w=[yv])
        yield
        K.tt(ysq[:].rearrange("p (h v) -> p h v", h=8), Vtok[:, tc * 512:(tc + 1) * 512].rearrange("p (h v) -> p h v", h=8),
             cbs[:, tc * 8:(tc + 1) * 8].unsqueeze(2).to_broadcast([128, 8, 64]), ALU.mult, r=[Vtok, cbs], w=[ysq])
        K.tt(yv[:], yv[:], ysq[:], ALU.add, r=[yv, ysq], w=[yv])
        yield
        for c in range(4):
            K.tr(PTf[:, c * 128:(c + 1) * 128], yv[:, c * 128:(c + 1) * 128], identf[:], r=[yv, identf], w=[PT])
        yo_ = yaT[tc % 2]
        K.tt(yo_[:].rearrange("p (c t) -> p c t", c=4), PTf.rearrange("p (c t) -> p c t", c=4),
             gA[:].rearrange("p (c t) -> p c t", c=4)[:, :, tc * 128:(tc + 1) * 128], ALU.mult, r=[PT, gA], w=[yo_])
        K.store(yT_d[0:512, t0 + tc * 128: t0 + (tc + 1) * 128].rearrange("(c p) t -> p c t", p=128),
                yo_[:].rearrange("p (c t) -> p c t", c=4), r=[yo_])
        yield

    def head_gen(ti):
        t0 = ti * 512
        hT = hTs[ti % 2]
        cosT, sinT = cosTs[ti % 2], sinTs[ti % 2]
        for sub in range(4):
            i = nx[0] % 2
            nx[0] += 1
            xx = xt[i]
            stt_ = stat[i]
            K.dma(xx[:], x_d[t0 + sub * 128: t0 + (sub + 1) * 128, :], w=[xx])
            K.memset(stt_[:], 0.0, w=[stt_])
            K.act(hb[i][:], xx[:], AF.Square, r=[xx], w=[hb[i], stt_], accum=stt_[:, 0:1])
            K.act(stt_[:, 1:2], stt_[:, 0:1], AF.Ln, r=[stt_], w=[stt_], scale=1.0 / D, bias=NORM_EPS)
            K.act(stt_[:, 2:3], stt_[:, 1:2], AF.Exp, r=[stt_], w=[stt_], scale=-0.5)
            K.act(hb[i][:], xx[:], AF.Identity, r=[xx, stt_], w=[hb[i]], scale=stt_[:, 2:3])
            for kc in range(8):
                K.tr(PT[:, kc * 128:(kc + 1) * 128], hb[i][:, kc * 128:(kc + 1) * 128], identb[:], r=[hb[i], identb], w=[PT])
            K.cpa(hT[:].rearrange("p (k t) -> p k t", k=8)[:, :, sub * 128:(sub + 1) * 128],
                  PT[:].rearrange("p (k t) -> p k t", k=8), r=[PT], w=[hT])
            yield
        for base in (0, 64):
            K.dma(cosT[base:base + 16, :], cos_d[:, t0:t0 + 512], w=[cosT])
            K.dma(sinT[base:base + 16, :], sin_d[:, t0:t0 + 512], w=[sinT])
        if do_rwkv:
            wb = wch[pj[0] % 4]
            pj[0] += 1
            pb = PB[6]
            K.dma(wb[:], wsc_d[12], r=[wscB], w=[wb])
            for kc in range(8):
                K.mm(pb[:], wb[:, kc * 128:(kc + 1) * 128], hT[:, kc * 512:(kc + 1) * 512],
                     start=(kc == 0), stop=(kc == 7), r=[wb, hT], w=[pb])
            shift_evac(pb, 12, was)
            K.act(twx[0:64, :], was[0:64, :], AF.Exp, r=[was], w=[twx], scale=2.0)
            K.act(twx[0:64, :], twx[0:64, :], AF.Ln, r=[twx], w=[twx], bias=1.0)
            K.act(twx[0:64, :], twx[0:64, :], AF.Exp, r=[twx], w=[twx], scale=-1.0)
            K.ts(twx[0:64, :], twx[0:64, :], -2.0, 1.0, ALU.mult, ALU.add, r=[twx], w=[twx])
            yield
            for tc in range(4):
                pz = PB[6]
                K.mm(pz[:], twx[0:65, tc * 128:(tc + 1) * 128], dupx[0:65, :], r=[twx, dupx], w=[pz])
                dst = sg[:, tc * 512:(tc + 1) * 512]
                K.act(dst, pz[:], AF.Exp, r=[pz], w=[sg], scale=-1.0)
                sigmoid_from_exp(dst, r=[sg], w=[sg])
                yield

    hd = [head_gen(0)]

    def hd_step(n=1):
        for _ in range(n):
            if hd[0] is None:
                return
            try:
                next(hd[0])
            except StopIteration:
                hd[0] = None
                return

    for ti in range(n_tiles):
        t0 = ti * 512
        hd_step(1000)
        tb['hT'] = hTs[ti % 2]
        qg[0] = q_gen(t0, hTs[ti % 2], cosTs[ti % 2], sinTs[ti % 2])
        if ti + 1 < n_tiles:
            hd[0] = head_gen(ti + 1)
        if do_rwkv:
            if DBG[0] == 50:
                return
            for c in range(4):
                i2 = c % 2
                pb = proj(c)
                shift_evac(pb, c, rs[i2])
                pb = proj(4 + c)
                shift_evac(pb, 4 + c, ks[i2])
                pb = proj(8 + c)
                shift_evac(pb, 8 + c, vs[i2])
                r_, k_, v_ = rs[i2], ks[i2], vs[i2]
                pza = PB[2]
                K.mm(pza[:], iup[64:128, c * 128:(c + 1) * 128], was[64:128, :], r=[iup, was], w=[pza])
                K.act(av[:], pza[:], AF.Exp, r=[pza, nib], w=[av], scale=-1.0, bias=nib[:, c:c + 1])
                sigmoid_from_exp(av[:], r=[av], w=[av])
                q_step(1)
                if DBG[0] == 51:
                    continue
                for tc in range(4):
                    pcs = PB[3 + tc % 2]
                    K.mm(pcs[:, 0:259], sg[:, tc * 512 + c * 128: tc * 512 + (c + 1) * 128], cm[:, 0:259], r=[sg, cm], w=[pcs])
                    K.act(E1[:, tc * 128:(tc + 1) * 128], pcs[:, 0:128], AF.Exp, r=[pcs], w=[E1])
                    K.act(E2[:, tc * 128:(tc + 1) * 128], pcs[:, 128:256], AF.Exp, r=[pcs], w=[E2])
                    K.act(E3[:, tc * 128:(tc + 1) * 128], pcs[:, 0:128], AF.Exp, r=[pcs], w=[E3], scale=-1.0)
                    o = (c * 4 + tc) * 3
                    K.act(decs[:, o:o + 3], pcs[:, 256:259], AF.Exp, r=[pcs], w=[decs])
                if DBG[0] == 52:
                    continue
                K.act(sv[:], k_[:], AF.Identity, r=[k_, vecs], w=[sv], scale=vecs[:, KNS + c:KNS + c + 1])
                K.act(sq[:], sv[:], AF.Square, r=[sv], w=[sq])
                pss = PB[2]
                K.mm(pss[:], bones[:], sq[:], r=[bones, sq], w=[pss])
                K.act(sq[:], pss[:], AF.Ln, r=[pss], w=[sq], bias=1e-30)
                K.act(sq[:], sq[:], AF.Exp, r=[sq], w=[sq], scale=-0.5)
                K.tt(kkv[:], sv[:], sq[:], ALU.mult, r=[sv, sq], w=[kkv])
                q_step(1)
                K.act(t1[:], av[:], AF.Identity, r=[av, vecs, nkim], w=[t1], scale=vecs[:, KIM + c:KIM + c + 1], bias=nkim[:, c:c + 1])
                K.stt(kp[:], t1[:], 1.0, k_[:], ALU.add, ALU.mult, r=[t1, k_], w=[kp])
                K.tt(bv[:], kkv[:], av[:], ALU.mult, r=[kkv, av], w=[bv])
                K.stt(rkb[:], r_[:], vecs[:, BON + c:BON + c + 1], kp[:], ALU.mult, ALU.mult, r=[r_, vecs, kp], w=[rkb])
                pbon = PB[5]
                for tc in range(4):
                    K.mm(pbon[:, tc * 8 + 2 * c: tc * 8 + 2 * c + 2], rkb[:, tc * 128:(tc + 1) * 128], hsel[:, c * 8 + 2 * c: c * 8 + 2 * c + 2],
                         r=[rkb, hsel], w=[pbon])
                if DBG[0] == 53:
                    continue
                a4 = arT[c][:].rearrange("p (tc w t) -> p tc w t", tc=4, w=2)
                k4 = kbT[c][:].rearrange("p (tc w t) -> p tc w t", tc=4, w=2)
                v3 = lambda tl: tl[:].rearrange("p (tc t) -> p tc t", tc=4)
                K.stt(a4[:, :, 0, :], v3(kkv), -1.0, v3(E2), ALU.mult, ALU.mult, r=[kkv, E2], w=[arT[c]])
                K.tt(a4[:, :, 1, :], v3(r_), v3(E1), ALU.mult, r=[r_, E1], w=[arT[c]])
                K.tt(k4[:, :, 0, :], v3(kp), v3(E3), ALU.mult, r=[kp, E3], w=[kbT[c]])
                K.tt(k4[:, :, 1, :], v3(bv), v3(E3), ALU.mult, r=[bv, E3], w=[kbT[c]])
                K.cp(vbf[:], v_[:], r=[v_], w=[vbf], eng='scalar')
                for which, dstT in ((0, Ktok), (1, Btok), (2, Vtok)):
                    half = (c * 3 + which) % 2
                    pt = PT[:, half * 512:(half + 1) * 512]
                    for tc in range(4):
                        if which < 2:
                            src = kbT[c][:, tc * 256 + which * 128: tc * 256 + (which + 1) * 128]
                            rr = [kbT[c], identb]
                        else:
                            src = vbf[:, tc * 128:(tc + 1) * 128]
                            rr = [vbf, identb]
                        K.tr(pt[:, tc * 128:(tc + 1) * 128], src, identb[:], r=rr, w=[PT])
                    K.cpa(dstT[:].rearrange("p (tc ch) -> p tc ch", tc=4)[:, :, c * 128:(c + 1) * 128],
                          pt.rearrange("p (tc ch) -> p tc ch", tc=4), r=[PT], w=[dstT])
                pb = proj(13 + c)
                gdst = gA[:, c * 512:(c + 1) * 512]
                K.act(kp[:], pb[:], AF.Exp, r=[pb], w=[kp], scale=-1.0)
                sigmoid_from_exp(kp[:], r=[kp], w=[kp])
                K.tt(gdst, kp[:], pb[:], ALU.mult, r=[kp, pb], w=[gA])
                q_step(1)
            if DBG[0] in (51, 52, 53):
                return
            K.cp(cbs[:], PB[5][:, 0:32], r=[PB[5]], w=[cbs])

            if DBG[0] == 54:
                return
            for tc in range(4):
                for c in range(4):
                    pH = [PB[(c % 2) * 2], PB[(c % 2) * 2 + 1]]
                    for half in range(2):
                        hs = slice(half * 64, half * 64 + 64)
                        ar = arT[c][hs, tc * 256:(tc + 1) * 256]
                        K.mm(pH[half][:, 0:256], kbT[c][hs, tc * 256 + 128: tc * 256 + 256], ar, r=[kbT[c], arT[c]], w=[pH[half]])
                        K.mm(pH[half][:, 256:512], kbT[c][hs, tc * 256: tc * 256 + 128], ar, r=[kbT[c], arT[c]], w=[pH[half]])
                    for half in range(2):
                        K.tt(ABm[c][:, half * 256:(half + 1) * 256], pH[half][:, 0:256], mask2[:, 0:256], ALU.mult, r=[pH[half], mask2], w=[ABm[c]])
                        K.tt(AKm[c][:, half * 256:(half + 1) * 256], pH[half][:, 256:512], mask2[:, 0:256], ALU.mult, r=[pH[half], mask2], w=[AKm[c]])
                if DBG[0] in (55, 551, 552):
                    continue
                q_step(1)
                gn_step(1)
                for g in range(2):
                    pNh = [PB[4], PB[5]]
                    for hh in range(4):
                        h = g * 4 + hh
                        c, half = h // 2, h % 2
                        hs = slice(half * 64, half * 64 + 64)
                        col = g * 256 + (hh // 2) * 128
                        K.mm(pNh[half][:, col:col + 128], arT[c][hs, tc * 256: tc * 256 + 128], kbT[c][hs, tc * 256 + 128: tc * 256 + 256],
                             r=[arT[c], kbT[c]], w=[pNh[half]])
                    for half in range(2):
                        K.tt(Nc[g][0][:].rearrange("p (pr h t) -> p pr h t", pr=2, h=2)[:, :, half, :],
                             pNh[half][:, g * 256:(g + 1) * 256].rearrange("p (pr t) -> p pr t", pr=2),
                             maskN[:, 0:256].rearrange("p (pr t) -> p pr t", pr=2), ALU.mult, r=[pNh[half], maskN], w=[Nc[g][0]])
                    for pr in range(2):
                        c = g * 2 + pr
                        src = ABm[c][:].rearrange("p (h w t) -> p h w t", h=2, w=2)[:, :, 0, :]
                        K.tt(Pc[g][0][:, pr * 256:(pr + 1) * 256].rearrange("p (h t) -> p h t", h=2), src,
                             identb[:].unsqueeze(1).to_broadcast([128, 2, 128]), ALU.add, r=[ABm[c], identb], w=[Pc[g][0]])
                if DBG[0] == 56:
                    continue
                def Mop(g, j, hh):
                    if j == 0:
                        h = g * 4 + hh
                        c, half = h // 2, h % 2
                        return ABm[c][:, half * 256: half * 256 + 128], ABm[c]
                    return Mc[g][j % 2][:, hh * 128:(hh + 1) * 128], Mc[g][j % 2]

                for j in range(6):
                    q_step(1)
                    gn_step(1)
                    if tc >= 1:
                        hd_step(1)
                    cur, nxt = j % 2, (j + 1) % 2
                    pNn = [PB[0], PB[3]]
                    pMn = [PB[1], PB[4]]
                    pP = [PB[2], PB[5]]
                    for g in range(2):
                        for hh in range(4):
                            sl = slice(hh * 128, (hh + 1) * 128)
                            m_ap, m_tl = Mop(g, j, hh)
                            K.mm(pNn[g][:, sl], m_ap, Nc[g][cur][:, sl], r=[m_tl, Nc[g][cur]], w=[pNn[g]])
                        if j < 5:
                            for hh in range(4):
                                sl = slice(hh * 128, (hh + 1) * 128)
                                m_ap, m_tl = Mop(g, j, hh)
                                K.mm(pMn[g][:, sl], Nc[g][cur][:, sl], m_ap, r=[m_tl, Nc[g][cur]], w=[pMn[g]])
                    for g in range(2):
                        K.cp(Nc[g][nxt][:], pNn[g][:], r=[pNn[g]], w=[Nc[g][nxt]], eng='scalar')
                        if j < 5:
                            K.cp(Mc[g][nxt][:], pMn[g][:], r=[pMn[g]], w=[Mc[g][nxt]], eng=('vector' if g == 0 else 'scalar'))
                    for g in range(2):
                        for hh in range(4):
                            sl = slice(hh * 128, (hh + 1) * 128)
                            K.mm(pP[g][:, sl], identb[:], Pc[g][cur][:, sl], start=True, stop=False, r=[identb, Pc[g][cur]], w=[pP[g]])
                            K.mm(pP[g][:, sl], Nc[g][nxt][:, sl], Pc[g][cur][:, sl], start=False, stop=True, r=[Nc[g][nxt], Pc[g][cur]], w=[pP[g]])
                    K.cp(Pc[0][nxt][:], pP[0][:], r=[pP[0]], w=[Pc[0][nxt]], eng='scalar')
                    K.cp(Pc[1][nxt][:], pP[1][:], r=[pP[1]], w=[Pc[1][nxt]], eng='vector')
                if DBG[0] == 57:
                    continue
                TTt = [Pc[0][0], Pc[1][0]]
                gn_step(1000)
                d4 = decs[:].rearrange("p (c tc k) -> p c tc k", c=4, tc=4)
                St3 = St[:].rearrange("p (c v) -> p c v", c=4)
                K.tt(St0m[:].rearrange("p (c v) -> p c v", c=4), St3, d4[:, :, tc, 0:1].to_broadcast([128, 4, 64]), ALU.mult,
                     r=[St, decs], w=[St0m])
                pX, pU, pY, pS = PB[0], PB[1], PB[2], PB[3]
                for h in range(8):
                    c, half = h // 2, h % 2
                    hs = slice(half * 64, half * 64 + 64)
                    vh = Vtok[:, tc * 512 + h * 64: tc * 512 + (h + 1) * 64]
                    K.mm(pX[:, h * 64:(h + 1) * 64], arT[c][hs, tc * 256: tc * 256 + 128], St0m[hs, c * 64:(c + 1) * 64],
                         start=True, stop=False, r=[arT[c], St0m], w=[pX])
                    K.mm(pX[:, h * 64:(h + 1) * 64], AKm[c][:, half * 256: half * 256 + 128], vh,
                         start=False, stop=True, r=[AKm[c], Vtok], w=[pX])
                K.cp(Xb[:], pX[:], r=[pX], w=[Xb], eng='scalar')
                q_step(1)
                for h in range(8):
                    g, hh = h // 4, h % 4
                    K.mm(pU[:, h * 64:(h + 1) * 64], TTt[g][:, hh * 128:(hh + 1) * 128], Xb[:, h * 64:(h + 1) * 64], r=[TTt[g], Xb], w=[pU])
                K.cp(Ub[:], pU[:], r=[pU], w=[Ub], eng='vector')
                q_step(1)
                for h in range(8):
                    c, half = h // 2, h % 2
                    hs = slice(half * 64, half * 64 + 64)
                    vh = Vtok[:, tc * 512 + h * 64: tc * 512 + (h + 1) * 64]
                    uh = Ub[:, h * 64:(h + 1) * 64]
                    yo = pY[:, h * 64:(h + 1) * 64]
                    K.mm(yo, arT[c][hs, tc * 256 + 128: tc * 256 + 256], St0m[hs, c * 64:(c + 1) * 64], start=True, stop=False, r=[arT[c], St0m], w=[pY])
                    K.mm(yo, ABm[c][:, half * 256 + 128: half * 256 + 256], uh, start=False, stop=False, r=[ABm[c], Ub], w=[pY])
                    K.mm(yo, AKm[c][:, half * 256 + 128: half * 256 + 256], vh, start=False, stop=True, r=[AKm[c], Vtok], w=[pY])
                    so = pS[hs, c * 64:(c + 1) * 64]
                    K.mm(so, Btok[:, tc * 512 + h * 64: tc * 512 + (h + 1) * 64], uh, start=True, stop=False, r=[Btok, Ub], w=[pS])
                    K.mm(so, Ktok[:, tc * 512 + h * 64: tc * 512 + (h + 1) * 64], vh, start=False, stop=True, r=[Ktok, Vtok], w=[pS])
                K.tt(stmp[:].rearrange("p (c v) -> p c v", c=4), pS[:, 0:256].rearrange("p (c v) -> p c v", c=4),
                     d4[:, :, tc, 2:3].to_broadcast([128, 4, 64]), ALU.mult, r=[pS, decs], w=[stmp])
                K.tt(stmp2[:].rearrange("p (c v) -> p c v", c=4), St3, d4[:, :, tc, 1:2].to_broadcast([128, 4, 64]), ALU.mult,
                     r=[St, decs], w=[stmp2])
                K.tt(St[:], stmp[:], stmp2[:], ALU.add, r=[stmp, stmp2], w=[St])
                if DBG[0] == 58:
                    continue
                q_step(1)
                K.cp(yv[:], pY[:], r=[pY], w=[yv], eng='scalar')
                gn[0] = gn_gen(tc, t0)
            gn_step(1000)

        if 50 <= DBG[0] <= 59 or DBG[0] in (551, 552):
            return
        q_step(1000)


def phase2(nc, S, K, ctx, sb, PB, maskP_d, qT_d, kT_d, v_d, gB_d, yT_d):
    maskP = sb(ctx, "maskP", [128, 512], BF16)
    K.dma(maskP[:], maskP_d, w=[maskP])
    qp = [sb(ctx, "qp%d" % i, [128, T], BF16) for i in range(2)]
    kp = [sb(ctx, "kpp%d" % i, [128, T], BF16) for i in range(2)]
    gp = [sb(ctx, "gp%d" % i, [128, T], BF16) for i in range(2)]
    Vd = [[sb(ctx, "Vd%d_%d" % (i, d), [128, 32 * 128], BF16) for d in range(3)] for i in range(2)]
    for i in range(2):
        for d in range(3):
            K.memset(Vd[i][d][:].rearrange("p (b c) -> p b c", c=128)[:, :, (1 - i) * 64:(1 - i) * 64 + 64], 1.0, w=[Vd[i][d]])
    acc = sb(ctx, "acc", [128, T])
    Pt = [sb(ctx, "Pt%d" % i, [128, 512], BF16) for i in range(4)]
    rden = sb(ctx, "rden", [128, T])
    yb = [sb(ctx, "ybo%d" % i, [128, T], BF16) for i in range(2)]
    DIL = (1, 4, 16)

    def loads(h):
        c, half = h // 2, h % 2
        pi = c % 2
        if half == 0:
            K.dma(qp[pi][:], qT_d[c * 128:(c + 1) * 128, :], w=[qp[pi]])
            K.dma(kp[pi][:], kT_d[c * 128:(c + 1) * 128, :], w=[kp[pi]])
            K.dma(gp[pi][:], gB_d[c * 128:(c + 1) * 128, :], w=[gp[pi]])
        vi = h % 2
        for di, d in enumerate(DIL):
            nb = 32 // d
            V3 = Vd[vi][di][:].rearrange("p (b c) -> p b c", c=128)
            for r in range(d):
                src = v_d[:, h * 64:(h + 1) * 64].rearrange("(n j r) c -> r j n c", r=d, j=128)[r]
                K.dma(V3[:, r * nb:(r + 1) * nb, half * 64:half * 64 + 64], src, w=[Vd[vi][di]])

    SB5 = [PB[0], PB[1], PB[2], PB[3], PB[6]]
    for pbk in SB5:
        K.memset(pbk[:], 0.0, w=[pbk], eng='vector')
    jobs = []
    for h in range(8):
        for di, d in enumerate(DIL):
            nb = 32 // d
            blocks = [(r, n) for n in range(nb) for r in range(d)] if d > 1 else [(0, n) for n in range(32)]
            for gi in range(8):
                grp = blocks[gi * 4:(gi + 1) * 4]
                for pair in range(2):
                    jobs.append(dict(h=h, di=di, d=d, nb=nb, gi=gi, pair=pair, grp=grp))
    for k, jb in enumerate(jobs):
        jb['k'] = k
        jb['first'] = (k == 0 or jobs[k - 1]['h'] != jb['h'])
        jb['last'] = (k == len(jobs) - 1 or jobs[k + 1]['h'] != jb['h'])

    def stage_S(jb):
        h, d, k = jb['h'], jb['d'], jb['k']
        c, half = h // 2, h % 2
        hs = slice(half * 64, half * 64 + 64)
        pi = c % 2
        q_, k_ = qp[pi], kp[pi]
        pS = SB5[k % 5]
        for bi in range(2):
            r, n = jb['grp'][jb['pair'] * 2 + bi]
            qs = q_[hs, r + d * 128 * n: r + d * 128 * n + d * 127 + 1: d]
            if n > 0:
                kprev = k_[hs, r + d * 128 * (n - 1): r + d * 128 * (n - 1) + d * 127 + 1: d]
                K.mm(pS[:, bi * 256: bi * 256 + 128], kprev, qs, r=[k_, q_], w=[pS])
            kcur = k_[hs, r + d * 128 * n: r + d * 128 * n + d * 127 + 1: d]
            K.mm(pS[:, bi * 256 + 128: bi * 256 + 256], kcur, qs, r=[k_, q_], w=[pS])

    def stage_E(jb):
        k = jb['k']
        pS = SB5[k % 5]
        P_ = Pt[k % 4]
        grp, pair = jb['grp'], jb['pair']
        K.act(P_[:], pS[:], AF.Exp, r=[pS], w=[P_], scale=0.125)
        K.tt(P_[:], P_[:], maskP[:], ALU.mult, r=[P_, maskP], w=[P_])

    def stage_V(jb):
        h, d, k, di, nb, gi = jb['h'], jb['d'], jb['k'], jb['di'], jb['nb'], jb['gi']
        c, half = h // 2, h % 2
        hs = slice(half * 64, half * 64 + 64)
        pi = c % 2
        vi = h % 2
        V3 = Vd[vi][di][:].rearrange("p (b c) -> p b c", c=128)
        P_ = Pt[k % 4]
        po = PB[4 + gi % 2]
        grp, pair = jb['grp'], jb['pair']
        for bi in range(2):
            r, n = grp[pair * 2 + bi]
            slot = pair * 2 + bi
            oo = po[:, slot * 128:(slot + 1) * 128]
            blk = r * nb + n
            if n > 0:
                K.mm(oo, V3[:, blk - 1, :], P_[:, bi * 256: bi * 256 + 128], start=True, stop=False, r=[Vd[vi][di], P_], w=[po])
            K.mm(oo, V3[:, blk, :], P_[:, bi * 256 + 128: bi * 256 + 256], start=(n == 0), stop=True, r=[Vd[vi][di], P_], w=[po])

    def stage_A(jb):
        h, d, k, di, nb, gi = jb['h'], jb['d'], jb['k'], jb['di'], jb['nb'], jb['gi']
        c, half = h // 2, h % 2
        hs = slice(half * 64, half * 64 + 64)
        pi = c % 2
        po = PB[4 + gi % 2]
        grp, pair = jb['grp'], jb['pair']
        if pair == 1:
            if d == 1:
                K.cp(acc[:, gi * 512:(gi + 1) * 512], po[:], r=[po], w=[acc], eng='scalar')
            else:
                r0, n = grp[0]
                span = d * 128
                av = acc[:, n * span:(n + 1) * span].rearrange("p (i r) -> p r i", r=d)[:, r0:r0 + 4, :]
                K.tt(av, av, po[:].rearrange("p (r i) -> p r i", r=4), ALU.add, r=[acc, po], w=[acc])
        if jb['last']:
            os_ = slice((1 - half) * 64, (1 - half) * 64 + 64)
            K.recip(rden[hs, :], acc[os_, :], r=[acc], w=[rden])
            K.tt(acc[hs, :], acc[hs, :], rden[hs, :], ALU.mult, r=[acc, rden], w=[acc])
            K.tt(yb[pi][hs, :], acc[hs, :], gp[pi][hs, :], ALU.mult, r=[acc, gp[pi]], w=[yb[pi]])
            if half == 1:
                K.store(yT_d[512 + c * 128: 512 + (c + 1) * 128, :], yb[pi][:], r=[yb[pi]])
            if h + 2 < 8:
                loads(h + 2)

    DEPTH = 3
    LAG = 2
    nj = len(jobs)
    loads(0)
    loads(1)
    for k in range(min(DEPTH, nj)):
        stage_S(jobs[k])
    for k in range(nj):
        if k + DEPTH < nj:
            stage_S(jobs[k + DEPTH])
        stage_E(jobs[k])
        stage_V(jobs[k])
        if k - LAG >= 0:
            stage_A(jobs[k - LAG])
    for k in range(max(0, nj - LAG), nj):
        stage_A(jobs[k])


def phase3(nc, S, K, ctx, sb, PB, w3, x_d, yT_d, out_d):
    woutb, fg = w3
    yt = [sb(ctx, "yt%d" % i, [128, 8 * 128], BF16) for i in range(3)]
    xr = [sb(ctx, "xr%d" % i, [128, 1024]) for i in range(3)]
    xo = [sb(ctx, "xo%d" % i, [128, 1024]) for i in range(2)]
    oo = [sb(ctx, "oo%d" % i, [128, 1024]) for i in range(2)]
    stat = [sb(ctx, "stat3_%d" % i, [128, 4]) for i in range(2)]
    for t in range(32):
        i3, i2 = t % 3, t % 2
        K.dma(yt[i3][:].rearrange("p (c t) -> p c t", c=8), yT_d[:, t * 128:(t + 1) * 128].rearrange("(c p) t -> p c t", p=128), w=[yt[i3]])
        K.dma(xr[i3][:], x_d[t * 128:(t + 1) * 128, :], w=[xr[i3]])
        for hf in range(2):
            pb = PB[(t * 2 + hf) % 4]
            for kc in range(8):
                K.mm(pb[:], yt[i3][:, kc * 128:(kc + 1) * 128], woutb[:, kc * 1024 + hf * 512: kc * 1024 + (hf + 1) * 512],
                     start=(kc == 0), stop=(kc == 7), r=[yt[i3], woutb], w=[pb])
            K.tt(xo[i2][:, hf * 512:(hf + 1) * 512], pb[:], xr[i3][:, hf * 512:(hf + 1) * 512], ALU.add, r=[pb, xr[i3]], w=[xo[i2]])
        st = stat[i2]
        K.memset(st[:], 0.0, w=[st])
        K.act(oo[i2][:], xo[i2][:], AF.Square, r=[xo[i2]], w=[oo[i2], st], accum=st[:, 0:1])
        K.act(st[:, 1:2], st[:, 0:1], AF.Ln, r=[st], w=[st], scale=1.0 / D, bias=NORM_EPS)
        K.act(st[:, 2:3], st[:, 1:2], AF.Exp, r=[st], w=[st], scale=-0.5)
        K.stt(oo[i2][:], xo[i2][:], st[:, 2:3], fg[:], ALU.mult, ALU.mult, r=[xo[i2], st, fg], w=[oo[i2]])
        K.store(out_d[t * 128:(t + 1) * 128, :], oo[i2][:], r=[oo[i2]])


def host_consts():
    bf = ml_dtypes.bfloat16
    c = {}
    c["ident_b"] = np.eye(128, dtype=np.float32).astype(bf)
    c["ident_f"] = np.eye(128, dtype=np.float32)
    rs = np.zeros((128, 128), np.float32)
    for p in range(128):
        m = p % 64
        if m < 8:
            rs[p + 8, p] = -1.0
        elif m < 16:
            rs[p - 8, p] = 1.0
    c["rswap"] = rs.astype(bf)
    j = np.arange(128)[:, None]
    i = np.arange(128)[None, :]
    U = (j >= i).astype(np.float32)
    L = (j <= i).astype(np.float32)
    c["maskP"] = np.concatenate([U, L, U, L], 1).astype(bf)
    SU = (j < i).astype(np.float32)
    UI = (j <= i).astype(np.float32)
    c["mask2"] = np.concatenate([SU, UI, SU, UI], 1)
    SL = (j > i).astype(np.float32)
    c["maskN"] = np.concatenate([SL, SL, SL, SL], 1)
    s = np.arange(128)[:, None]
    t = np.arange(128)[None, :]
    mid = (s <= 63).astype(np.float64)
    cm = np.zeros((128, 259), np.float64)
    cm[:, 0:128] = (s <= t) - mid
    cm[:, 128:256] = (s < t) - mid
    cm[:, 256:257] = mid
    cm[:, 257] = 1.0
    cm[:, 258:259] = 1.0 - mid
    c["cm"] = (cm * NEG_E).astype(np.float32)
    p = np.arange(128)
    c["bones"] = (p[:, None] // 64 == p[None, :] // 64).astype(np.float32)
    hsel = np.zeros((128, 4, 8), np.float32)
    for cc in range(4):
        for pp in range(128):
            hsel[pp, cc, 2 * cc + pp // 64] = 1.0
    c["hsel"] = hsel.reshape(128, 32)
    half = 8
    inv = (np.float32(500000.0) ** (-(np.arange(half, dtype=np.float32) * np.float32(2.0) / np.float32(16)))).astype(np.float32)
    ang = (np.arange(T, dtype=np.float32)[None, :] * inv[:, None]).astype(np.float32)
    cs = np.cos(ang).astype(np.float32)
    sn = np.sin(ang).astype(np.float32)
    c["cos_t"] = np.concatenate([cs, cs], 0)
    c["sin_t"] = np.concatenate([sn, sn], 0)
    return c


def pc(v, nchunk):
    return np.ascontiguousarray(np.asarray(v, np.float32).reshape(nchunk, 128).T)


def make_in_maps(inputs):
    c = host_consts()
    vecs = np.zeros((128, 50), np.float32)
    vecs[:, 0:8] = pc(inputs["norm_gain"][0], 8)
    vecs[:, 8:21] = pc(inputs["shift_mix"][0], 13)
    vecs[:, 34:38] = pc(inputs["iclr_base"][0], 4)
    vecs[:, 38:42] = pc(inputs["key_norm_scale"][0], 4)
    vecs[:, 42:46] = pc(inputs["key_iclr_mix"][0], 4)
    vecs[:, 46:50] = pc(inputs["bonus"][0].reshape(-1), 4)
    shared = dict(c)
    shared["w_in"] = np.ascontiguousarray(inputs["w_in"][0], np.float32)
    shared["w_out"] = np.ascontiguousarray(inputs["w_out"][0], np.float32)
    shared["vecs"] = vecs
    shared["dupx"] = np.ascontiguousarray(np.concatenate([inputs["decay_up"][0], inputs["decay_base"][0][None, :]], 0), np.float32)
    shared["iclr_up"] = np.ascontiguousarray(inputs["iclr_up"][0], np.float32)
    shared["gn_g"] = np.ascontiguousarray(np.broadcast_to(inputs["gn_gain"][0][None, :], (128, 512)), np.float32)
    shared["gn_b"] = np.ascontiguousarray(np.broadcast_to(inputs["gn_bias"][0][None, :], (128, 512)), np.float32)
    shared["fgain"] = np.ascontiguousarray(np.broadcast_to(np.asarray(inputs["final_gain"])[None, :], (128, 1024)), np.float32)
    maps = []
    for b in range(8):
        m = dict(shared)
        m["x"] = np.ascontiguousarray(inputs["x"][b], np.float32)
        maps.append(m)
    return maps


def kernel(**inputs):
    inputs = {k: np.asarray(v) for k, v in inputs.items()}
    nc = build_program()
    maps = make_in_maps(inputs)
    res = run_bass_kernel_spmd(nc, maps, core_ids=list(range(8)))
    out = np.stack([np.asarray(res.results[b]["out"], np.float32).reshape(T, D) for b in range(8)], 0)
    return out
```
